# Optimizing a Trainium2 kernel written in Bass

```python
import jax, jax.numpy as jnp
from jax import lax
import numpy as np

D_MODEL = 1024
BATCH = 16
SEQ = 4096
DEPTH = 1
DEC_BATCH = 1
DEC_SEQ = 16384
PAST_LEN = 128

GRID_W = 64
MLA_HEADS = 8
QK_NOPE = 64
QK_ROPE = 32
V_HEAD = 64
Q_LORA = 768
KV_LORA = 256
ROPE_BASE = 10000.0
Q_BLOCK = 128
NA_HEADS = 8
NA_HD = 64
NA_WIN_R = 8
NA_WIN_C = 16
D_FF = 2816
CONV_W = 3
EPS = 1e-6

MLA_W = MLA_HEADS * V_HEAD
NA_W = NA_HEADS * NA_HD
SPLITS = (Q_LORA, KV_LORA, QK_ROPE, NA_W, NA_W, NA_W, 2 * D_MODEL)
IN_COLS = sum(SPLITS)

kernel_name = 'hybrid_mla_natten_convffn_encoder'


def rmsnorm(x, g):
    xf = x.astype(jnp.float32)
    y = xf * lax.rsqrt(jnp.mean(xf * xf, axis=-1, keepdims=True) + EPS)
    return (y * g.astype(jnp.float32)).astype(x.dtype)


def rope_tables(S):
    inv = ROPE_BASE ** (-jnp.arange(0, QK_ROPE, 2, dtype=jnp.float32) / QK_ROPE)
    ang = jnp.arange(S, dtype=jnp.float32)[:, None] * inv[None, :]
    return jnp.cos(ang), jnp.sin(ang)


def rotary(x, cos, sin):
    x1, x2 = jnp.split(x, 2, axis=-1)
    c = cos[None, :, None, :].astype(x.dtype)
    s = sin[None, :, None, :].astype(x.dtype)
    return jnp.concatenate([x1 * c - x2 * s, x1 * s + x2 * c], axis=-1)


def mla_branch(c_q, c_kv, k_pe, g_q, w_qb, g_kv, w_kvb):
    B, S, _ = c_q.shape
    q = (rmsnorm(c_q, g_q) @ w_qb).reshape(B, S, MLA_HEADS, QK_NOPE + QK_ROPE)
    q_nope, q_pe = q[..., :QK_NOPE], q[..., QK_NOPE:]
    kv = (rmsnorm(c_kv, g_kv) @ w_kvb).reshape(B, S, MLA_HEADS, QK_NOPE + V_HEAD)
    k_nope, v = kv[..., :QK_NOPE], kv[..., QK_NOPE:]
    cos, sin = rope_tables(S)
    q_pe = rotary(q_pe, cos, sin)
    k_pe = rotary(k_pe[:, :, None, :], cos, sin)[:, :, 0, :]
    scale = (QK_NOPE + QK_ROPE) ** -0.5
    n_blk = S // Q_BLOCK
    qn_b = q_nope.reshape(B, n_blk, Q_BLOCK, MLA_HEADS, QK_NOPE).transpose(1, 0, 2, 3, 4)
    qp_b = q_pe.reshape(B, n_blk, Q_BLOCK, MLA_HEADS, QK_ROPE).transpose(1, 0, 2, 3, 4)

    def block(args):
        qn, qp = args
        s = (jnp.einsum('bqhd,bkhd->bhqk', qn, k_nope)
             + jnp.einsum('bqhr,bkr->bhqk', qp, k_pe))
        p = jax.nn.softmax(s.astype(jnp.float32) * scale, axis=-1).astype(v.dtype)
        return jnp.einsum('bhqk,bkhd->bqhd', p, v)

    o = lax.map(block, (qn_b, qp_b))
    return o.transpose(1, 0, 2, 3, 4).reshape(B, S, MLA_W)


def na_branch(q, k, v, rpb):
    B, S, _ = q.shape
    rows = S // GRID_W
    wr = min(NA_WIN_R, rows)
    wc = NA_WIN_C
    n_keys = wr * wc
    q = q.reshape(B, rows, GRID_W, NA_HEADS, NA_HD).transpose(1, 0, 2, 3, 4)
    k = k.reshape(B, S, NA_HEADS, NA_HD)
    v = v.reshape(B, S, NA_HEADS, NA_HD)
    r = jnp.arange(rows)
    c = jnp.arange(GRID_W)
    rs = jnp.clip(r - wr // 2, 0, rows - wr)
    cs = jnp.clip(c - wc // 2, 0, GRID_W - wc)
    key_r = rs[:, None] + jnp.arange(wr)[None, :]
    key_c = cs[:, None] + jnp.arange(wc)[None, :]
    idx = (key_r[:, None, :, None] * GRID_W + key_c[None, :, None, :]).reshape(rows, GRID_W, n_keys)
    d_row = key_r - r[:, None] + (NA_WIN_R - 1)
    d_col = key_c - c[:, None] + (NA_WIN_C - 1)
    scale = NA_HD ** -0.5

    def row(args):
        q_r, idx_r, dr_r = args
        k_g = k[:, idx_r]
        v_g = v[:, idx_r]
        s = jnp.einsum('bwhd,bwkhd->bhwk', q_r, k_g).astype(jnp.float32) * scale
        bias = rpb[:, dr_r][:, :, d_col]
        bias = bias.transpose(0, 2, 1, 3).reshape(NA_HEADS, GRID_W, n_keys)
        p = jax.nn.softmax(s + bias[None].astype(jnp.float32), axis=-1).astype(v.dtype)
        return jnp.einsum('bhwk,bwkhd->bwhd', p, v_g)

    o = lax.map(row, (q, idx, d_row))
    return o.transpose(1, 0, 2, 3, 4).reshape(B, S, NA_W)


def conv_ffn(h, w_up, conv_w, conv_b, w_down):
    u, g = jnp.split(h @ w_up, 2, axis=-1)
    gp = jnp.pad(g, ((0, 0), (1, 1), (0, 0)))
    g = gp[:, :-2] * conv_w[0] + gp[:, 1:-1] * conv_w[1] + gp[:, 2:] * conv_w[2] + conv_b
    return (jax.nn.silu(g) * u) @ w_down


def trunk(x, norm_mix, w_in, b_gate, norm_q, w_qb, norm_kv, w_kvb, rpb,
          w_br_a, w_br_b, w_o, norm_ffn, w_up, conv_w, conv_b, w_down, norm_final):
    B, S, _ = x.shape
    cuts = list(np.cumsum(SPLITS)[:-1])
    for l in range(DEPTH):
        h = rmsnorm(x, norm_mix[l])
        c_q, c_kv, k_pe, na_q, na_k, na_v, gate_logits = jnp.split(h @ w_in[l], cuts, axis=-1)
        o_a = mla_branch(c_q, c_kv, k_pe, norm_q[l], w_qb[l], norm_kv[l], w_kvb[l])
        o_b = na_branch(na_q, na_k, na_v, rpb[l])
        gates = jax.nn.sigmoid(gate_logits.reshape(B, S, 2, D_MODEL) + b_gate[l])
        merged = gates[:, :, 0] * (o_a @ w_br_a[l]) + gates[:, :, 1] * (o_b @ w_br_b[l])
        x = x + merged @ w_o[l]
        x = x + conv_ffn(rmsnorm(x, norm_ffn[l]), w_up[l], conv_w[l], conv_b[l], w_down[l])
    return rmsnorm(x, norm_final)


def setup_inputs(seed: int = 0) -> dict:
    key = jax.random.key(seed)
    ks = jax.random.split(key, 24)

    def nrm(k, shape, fan):
        return jax.random.normal(k, shape, jnp.float32) * (fan ** -0.5)

    def gain(k, shape):
        return 1.0 + 0.05 * jax.random.normal(k, shape, jnp.float32)

    return {
        'x_prompt': jax.random.normal(ks[0], (BATCH, SEQ, D_MODEL), jnp.float32),
        'x_sample': jax.random.normal(ks[1], (DEC_BATCH, DEC_SEQ, D_MODEL), jnp.float32),
        'norm_mix': gain(ks[2], (DEPTH, D_MODEL)),
        'w_in': nrm(ks[3], (DEPTH, D_MODEL, IN_COLS), D_MODEL),
        'b_gate': 0.1 * jax.random.normal(ks[4], (DEPTH, 2, D_MODEL), jnp.float32),
        'norm_q': gain(ks[5], (DEPTH, Q_LORA)),
        'w_qb': nrm(ks[6], (DEPTH, Q_LORA, MLA_HEADS * (QK_NOPE + QK_ROPE)), Q_LORA),
        'norm_kv': gain(ks[7], (DEPTH, KV_LORA)),
        'w_kvb': nrm(ks[8], (DEPTH, KV_LORA, MLA_HEADS * (QK_NOPE + V_HEAD)), KV_LORA),
        'rpb': 0.1 * jax.random.normal(ks[9], (DEPTH, NA_HEADS, 2 * NA_WIN_R - 1, 2 * NA_WIN_C - 1), jnp.float32),
        'w_br_a': nrm(ks[10], (DEPTH, MLA_W, D_MODEL), MLA_W),
        'w_br_b': nrm(ks[11], (DEPTH, NA_W, D_MODEL), NA_W),
        'w_o': nrm(ks[12], (DEPTH, D_MODEL, D_MODEL), 2 * D_MODEL),
        'norm_ffn': gain(ks[13], (DEPTH, D_MODEL)),
        'w_up': nrm(ks[14], (DEPTH, D_MODEL, 2 * D_FF), D_MODEL),
        'conv_w': nrm(ks[15], (DEPTH, CONV_W, D_FF), CONV_W),
        'conv_b': 0.02 * jax.random.normal(ks[16], (DEPTH, D_FF), jnp.float32),
        'w_down': nrm(ks[17], (DEPTH, D_FF, D_MODEL), D_FF),
        'norm_final': gain(ks[18], (D_MODEL,)),
    }


def reference(x_prompt, x_sample, norm_mix, w_in, b_gate, norm_q, w_qb, norm_kv, w_kvb, rpb,
              w_br_a, w_br_b, w_o, norm_ffn, w_up, conv_w, conv_b, w_down, norm_final):
    y_prompt = trunk(x_prompt, norm_mix, w_in, b_gate, norm_q, w_qb, norm_kv, w_kvb, rpb,
                     w_br_a, w_br_b, w_o, norm_ffn, w_up, conv_w, conv_b, w_down, norm_final)
    y_sample = trunk(x_sample, norm_mix, w_in, b_gate, norm_q, w_qb, norm_kv, w_kvb, rpb,
                     w_br_a, w_br_b, w_o, norm_ffn, w_up, conv_w, conv_b, w_down, norm_final)
    return (y_prompt, y_sample)
```

```python
import math
from contextlib import ExitStack

import numpy as np
import ml_dtypes

import concourse.bass as bass
import concourse.mybir as mybir
from concourse.bass_utils import run_bass_kernel_spmd

F32 = mybir.dt.float32
BF16 = mybir.dt.bfloat16
AF = mybir.ActivationFunctionType
ALU = mybir.AluOpType

NCORES = 8
D = 1024
SP_LEN = 4096
SS_LEN = 16384
OWN = SS_LEN // NCORES
HALO_LO = 512
XO_LEN = 3072
FOFF = HALO_LO // 64
GRID_W = 64
H = 8
IN_COLS = 4640
DFF = 2816
NFF = DFF // 128
EPS = 1e-6
MASKV = -30000.0
KC = 1024
NBLK = KC // 128


class Ev:
    __slots__ = ("sem", "val", "clock")

    def __init__(self, sem, val, clock):
        self.sem, self.val, self.clock = sem, val, clock


class Res:
    def __init__(self, name, dsem=None, multi=False):
        self.name = name
        self.w = {}
        self.r = {}
        self.dsem = dsem
        self.dcount = 0
        self.multi = multi
        self.alias = []
        self.scratch = False


class Eng:
    def __init__(self, kb, name, eng, sem, is_pe=False):
        self.kb, self.name, self.eng, self.sem = kb, name, eng, sem
        self.count = 0
        self.known = {}
        self.is_pe = is_pe
        self.pend_r, self.pend_w = [], []

    def _need(self, ev, need):
        if ev is None or self.known.get(ev.sem, 0) >= ev.val:
            return
        cur = need.get(ev.sem)
        if cur is None or cur.val < ev.val:
            need[ev.sem] = ev

    def _sync(self, reads, writes):
        need = {}
        for r in reads:
            for ev in r.w.values():
                self._need(ev, need)
        for w in writes:
            if w.scratch:
                continue
            if not w.multi:
                for ev in w.w.values():
                    self._need(ev, need)
            for ev in w.r.values():
                self._need(ev, need)
            for al in w.alias:
                for ev in al.w.values():
                    self._need(ev, need)
                for ev in al.r.values():
                    self._need(ev, need)
        for sem, ev in need.items():
            if self.is_pe and sem is self.sem:
                continue
            self.eng.wait_ge(sem, ev.val)
            if self.known.get(sem, 0) < ev.val:
                self.known[sem] = ev.val
            for s, v in ev.clock.items():
                if self.known.get(s, 0) < v:
                    self.known[s] = v

    def _flush(self, ev):
        for r in self.pend_r:
            r.r[ev.sem] = ev
        for w in self.pend_w:
            if w.multi:
                w.w[ev.sem] = ev
            else:
                w.w = {ev.sem: ev}
                w.r = {}
        self.pend_r, self.pend_w = [], []

    def op(self, fn, reads=(), writes=(), inc=True):
        self._sync(reads, writes)
        ins = fn(self.eng)
        self.pend_r += list(reads)
        self.pend_w += list(writes)
        if inc:
            self.count += 1
            ins.then_inc(self.sem, 1)
            self._flush(Ev(self.sem, self.count, dict(self.known)))
        return ins

    def dma(self, out, in_, sres, dres=None, load=True, semres=None, **kw):
        if load:
            reads, writes = ([dres] if dres is not None else []), [sres]
        else:
            reads, writes = [sres], ([dres] if dres is not None else [])
        self._sync(reads, writes)
        semres = semres or sres
        semres.dcount += 16
        self.eng.dma_start(out=out, in_=in_, **kw).then_inc(semres.dsem, 16)
        ev = Ev(semres.dsem, semres.dcount, dict(self.known))
        for r in reads:
            r.r[ev.sem] = ev
        for w in writes:
            if w.multi:
                w.w[ev.sem] = ev
            else:
                w.w = {ev.sem: ev}
                w.r = {}


class KB:
    def __init__(self, nc):
        self.nc = nc
        self.stack = ExitStack()
        self.all_res = []
        mk = lambda n, e, pe=False: Eng(self, n, e, self.stack.enter_context(nc.semaphore("s_" + n)), pe)
        self.pe = mk("pe", nc.tensor, True)
        self.act = mk("act", nc.scalar)
        self.dve = mk("dve", nc.vector)
        self.pool = mk("pool", nc.gpsimd)
        self.sp = mk("sp", nc.sync)
        self.engs = [self.pe, self.act, self.dve, self.pool, self.sp]
        self._rr = 0

    def sb(self, name, shape, dt, dma=False):
        t = self.stack.enter_context(self.nc.sbuf_tensor("sb_" + name, list(shape), dt))
        return t, self.res(name, dma)

    def ps(self, name):
        t = self.stack.enter_context(self.nc.psum_tensor(name, [128, 512], F32))
        return t, self.res(name)

    def res(self, name, dma=False, multi=False):
        ds = self.stack.enter_context(self.nc.semaphore("d_" + name)) if dma else None
        r = Res(name, ds, multi)
        self.all_res.append(r)
        return r

    def scale_cast(self, out_ap, in_ap, scal_ap, reads, writes):
        self._rr += 1
        if self._rr % 2:
            self.dve.op(lambda e: e.tensor_scalar(out=out_ap, in0=in_ap, scalar1=scal_ap, scalar2=None, op0=ALU.mult),
                        reads=reads, writes=writes)
        else:
            self.act.op(lambda e: e.activation(out=out_ap, in_=in_ap, func=AF.Copy, scale=scal_ap), reads=reads,
                        writes=writes)

    def barrier(self):
        for e in self.engs:
            for o in self.engs:
                if o is not e and o.count > e.known.get(o.sem, 0):
                    e.eng.wait_ge(o.sem, o.count)
                    e.known[o.sem] = o.count
            for r in self.all_res:
                if r.dsem is not None and r.dcount > e.known.get(r.dsem, 0):
                    e.eng.wait_ge(r.dsem, r.dcount)
                    e.known[r.dsem] = r.dcount

    def finish(self):
        e = self.sp
        for o in self.engs:
            if o is not e and o.count > e.known.get(o.sem, 0):
                e.eng.wait_ge(o.sem, o.count)
        for r in self.all_res:
            if r.dsem is not None and r.dcount > e.known.get(r.dsem, 0):
                e.eng.wait_ge(r.dsem, r.dcount)


def split_windows(n_tok, max_out=510):
    n = math.ceil(n_tok / max_out)
    base, rem = divmod(n_tok, n)
    out, s = [], 0
    for i in range(n):
        sz = base + (1 if i < rem else 0)
        out.append((s, s + sz))
        s += sz
    return out


def na_row_blocks(kind, r):
    if kind == "p":
        rows = SP_LEN // GRID_W
        rs = min(max(r - 4, 0), rows - 8)
        a0 = rs - (rs % 2)
        return list(range(a0, rs + 8, 2))
    lo, hi = r - 4, r + 3
    if 0 <= r < 4:
        hi = max(hi, 7)
    if 28 < r <= 31:
        lo = min(lo, 24)
    lo_f, hi_f = lo + FOFF, hi + FOFF
    a0 = lo_f - (lo_f % 2)
    return list(range(a0, hi_f + 1, 2))


def na_groups(kind):
    if kind == "p":
        return [list(range(g, g + 4)) for g in range(0, SP_LEN // GRID_W, 4)]
    return [list(range(g, g + 4)) for g in range(-4, OWN // GRID_W + 4, 4)]


def na_group_blocks(kind, rows4):
    b = set()
    for r in rows4:
        b |= set(na_row_blocks(kind, r))
    return sorted(b)


def na_tables(rpb, kind, core):
    rows = SP_LEN // GRID_W if kind == "p" else SS_LEN // GRID_W
    nloc = rows if kind == "p" else OWN // GRID_W
    qc = np.arange(GRID_W)
    cs = np.clip(qc - 8, 0, GRID_W - 16)

    def row_tab(r, a):
        rg = r if kind == "p" else core * nloc + r
        rs = min(max(rg - 4, 0), rows - 8)
        if not (0 <= rg < rows):
            rs = rg - 4
        ag = a if kind == "p" else a - FOFF + core * nloc
        if a not in na_row_blocks(kind, r):
            return "masked", None
        key = tuple((ag + i - rg, rs <= ag + i < rs + 8) for i in range(2))
        if kind == "s" and (0 <= r < 4 or 28 < r <= 31):
            key = ("edge", r, a)

        def build():
            t = np.full((128, H, GRID_W), MASKV, np.float32)
            for i in range(2):
                kr = ag + i
                if not (rs <= kr < rs + 8):
                    continue
                dr = kr - rg + 7
                for kcol in range(GRID_W):
                    ok = (kcol >= cs) & (kcol < cs + 16)
                    dc = kcol - qc + 15
                    t[i * 64 + kcol][:, ok] = rpb[:, dr, dc[ok]]
            return t
        return key, build

    tabs, index, cache, rcache = [], {}, {}, {}
    for rows4 in na_groups(kind):
        for a in na_group_blocks(kind, rows4):
            parts = [row_tab(r, a) for r in rows4]
            key = tuple((r - rows4[0], k_) for r, (k_, _) in zip(rows4, parts))
            if key not in cache:
                t4 = np.full((128, H, 4 * GRID_W), MASKV, np.float32)
                for i, (k_, build) in enumerate(parts):
                    if build is not None:
                        if k_ not in rcache:
                            rcache[k_] = build()
                        t4[:, :, i * 64:(i + 1) * 64] = rcache[k_]
                cache[key] = len(tabs)
                tabs.append(t4)
            index[(rows4[0], a)] = cache[key]
    return np.stack(tabs).astype(ml_dtypes.bfloat16), index


def na_table_index(kind):
    raise NotImplementedError


def build_program(n_ptab, ptab_index, n_stab, stab_index, debug=None):
    nc = bass.Bass("TRN2", target_bir_lowering=False)
    kb = KB(nc)
    pe, act, dve, pool, sp = kb.pe, kb.act, kb.dve, kb.pool, kb.sp

    def din(name, shape, dt=F32):
        return nc.dram_tensor(name, list(shape), dt, kind="ExternalInput").ap()

    def dout(name, shape, dt=F32):
        return nc.dram_tensor(name, list(shape), dt, kind="ExternalOutput").ap()

    def dscr(name, shape, dt=BF16):
        kind = "ExternalOutput" if (debug and name in debug) else "Internal"
        return nc.dram_tensor(name, list(shape), dt, kind=kind).ap()

    xp = din("xp", [2, SP_LEN, D])
    xs = din("xs", [SS_LEN, D])
    xo = din("xo", [XO_LEN, D])
    w_in = din("w_in", [D, IN_COLS])
    w_qb = din("w_qb", [768, 768])
    w_kvb = din("w_kvb", [256, 1024])
    w_br_a = din("w_br_a", [512, D])
    w_br_b = din("w_br_b", [512, D])
    w_o = din("w_o", [D, D])
    w_up = din("w_up", [D, 2 * DFF])
    w_down = din("w_down", [DFF, D])
    norm_mix = din("norm_mix", [D])
    norm_q = din("norm_q", [768])
    norm_kv = din("norm_kv", [256])
    norm_ffn = din("norm_ffn", [D])
    norm_final = din("norm_final", [D])
    b_gate = din("b_gate", [2, D])
    conv_w = din("conv_w", [3, DFF])
    conv_b = din("conv_b", [DFF])
    ropek = din("ropek", [SS_LEN, 64])
    ropeq_p = din("ropeq_p", [64, SP_LEN])
    ropeq_s = din("ropeq_s", [64, XO_LEN])
    ptab = din("ptab", [n_ptab, 128, H, 256], BF16)
    stab = din("stab", [n_stab, 128, H, 256], BF16)
    hmask = din("hmask", [128, 2])
    ident_d = din("ident", [128, 128], BF16)

    yp = dout("yp", [2, SP_LEN, D])
    yo = dout("yo", [OWN, D])

    WIN = dscr("WIN", [D, IN_COLS])
    WQB = dscr("WQB", [768, 1024])
    WKVB = dscr("WKVB", [256, 1024])
    WBA = dscr("WBA", [512, D])
    WBB = dscr("WBB", [512, D])
    WO = dscr("WO", [D, D])
    WUP = dscr("WUP", [D, 2 * DFF])
    WDN = dscr("WDN", [DFF, D])
    r_w = kb.res("wscr", multi=True)

    class Seq:
        pass

    seqs = {}
    for nm, S_kv, T_na in (("p0", SP_LEN, SP_LEN), ("p1", SP_LEN, SP_LEN), ("sm", SS_LEN, 0), ("so", 0, XO_LEN)):
        s = Seq()
        s.name = nm
        if S_kv:
            s.KN = dscr("KN_" + nm, [512, S_kv])
            s.KP = dscr("KP_" + nm, [32, S_kv])
            s.VD = dscr("VD_" + nm, [H, 128, S_kv // 128, 128])
            s.r_kv = kb.res("kv_" + nm, multi=True)
        if T_na:
            s.NQ = dscr("NQ_" + nm, [512, T_na])
            s.NK = dscr("NK_" + nm, [512, T_na])
            s.NV = dscr("NV_" + nm, [T_na, 1024])
            s.OB = dscr("OB_" + nm, [512, T_na])
            s.r_na = kb.res("na_" + nm, multi=True)
            s.r_ob = kb.res("ob_" + nm, multi=True)
        seqs[nm] = s

    ident, r_ident = kb.sb("ident", [128, 128], BF16, dma=True)
    gvec, r_gvec = kb.sb("gvec", [128, 8 + 6 + 2 + 8], F32, dma=True)
    gsc, r_gsc = kb.sb("gsc", [128, 8 + 6], F32)
    bg, r_bg = kb.sb("bg", [128, 16], F32, dma=True)
    cw, r_cw = kb.sb("cw", [128, 4 * NFF], F32, dma=True)
    cwh, r_cwh = kb.sb("cwh", [128, 4 * NFF], F32)
    nfb, r_nfb = kb.sb("nfb", [128, D], F32, dma=True)
    epsT, r_eps = kb.sb("epsT", [128, 1], F32)
    hm, r_hm = kb.sb("hm", [128, 2], F32, dma=True)
    pdbl = [kb.stack.enter_context(nc.psum_tensor("pdbl%d" % i, [128, 1024], F32)) for i in range(4)]
    pbank = [(pdbl[i // 2][:, (i % 2) * 512:(i % 2 + 1) * 512], kb.res("pb%d" % i)) for i in range(8)]

    def PB(i):
        return pbank[i][0], pbank[i][1]

    def PBbf(i):
        return pbank[i][0].bitcast(BF16)

    sp.dma(ident[:], ident_d[:, :], r_ident)
    with nc.allow_non_contiguous_dma(reason="tiny one-time vector loads"):
        sp.dma(gvec[:, 0:8], norm_mix.rearrange("(c p) -> p c", p=128), r_gvec)
        sp.dma(gvec[:, 8:14], norm_q.rearrange("(c p) -> p c", p=128), r_gvec)
        sp.dma(gvec[:, 14:16], norm_kv.rearrange("(c p) -> p c", p=128), r_gvec)
        sp.dma(gvec[:, 16:24], norm_ffn.rearrange("(c p) -> p c", p=128), r_gvec)
        sp.dma(bg[:, :].rearrange("p (t c) -> p t c", t=2), b_gate.rearrange("t (c p) -> p t c", p=128), r_bg)
        sp.dma(cw[:, 0:3 * NFF].rearrange("p (t c) -> p t c", t=3), conv_w.rearrange("t (c p) -> p t c", p=128), r_cw)
        sp.dma(cw[:, 3 * NFF:4 * NFF], conv_b.rearrange("(c p) -> p c", p=128), r_cw)
    sp.dma(nfb[:], norm_final.partition_broadcast(128), r_nfb)
    sp.dma(hm[:], hmask[:, :], r_hm)
    dve.op(lambda e: e.memset(epsT[:], EPS), writes=[r_eps])
    dve.op(lambda e: e.tensor_scalar(out=gsc[:, 0:8], in0=gvec[:, 0:8], scalar1=0.125, scalar2=None, op0=ALU.mult),
           reads=[r_gvec], writes=[r_gsc])
    dve.op(lambda e: e.tensor_scalar(out=gsc[:, 8:14], in0=gvec[:, 8:14], scalar1=96.0 ** -0.5, scalar2=None,
                                     op0=ALU.mult), reads=[r_gvec], writes=[r_gsc])
    dve.op(lambda e: e.tensor_scalar(out=bg[:, :], in0=bg[:, :], scalar1=0.5, scalar2=None, op0=ALU.mult),
           reads=[r_bg], writes=[r_bg])
    dve.op(lambda e: e.tensor_copy(out=cwh[:, :], in_=cw[:, :]), reads=[r_cw], writes=[r_cwh])

    def make_staging(stk, tag, ns):
        si, so = [], []
        for i in range(ns):
            t = stk.enter_context(nc.sbuf_tensor("w0i%s%d" % (tag, i), [128, 2048], F32))
            si.append((t, kb.res("w0i%s%d" % (tag, i), dma=True)))
            t2 = stk.enter_context(nc.sbuf_tensor("w0o%s%d" % (tag, i), [128, 2048], BF16))
            so.append((t2, kb.res("w0o%s%d" % (tag, i), dma=True)))
        return si, so

    def prep(stg, cnt, src, dst, K, ncols, gain=None, const=None, blocks=None, special=None):
        stg_in, stg_out = stg
        ns = len(stg_in)
        blocks = blocks or [(c0, min(c0 + 2048, ncols)) for c0 in range(0, ncols, 2048)]
        for kc in range(K // 128):
            for (c0, c1, *extra) in blocks:
                i = cnt[0] % ns
                cnt[0] += 1
                ti, ri = stg_in[i]
                to, ro = stg_out[i]
                w = c1 - c0
                sp.dma(ti[:, 0:w], src[kc * 128:(kc + 1) * 128, c0:c1], ri)
                g = gain(kc, extra) if gain else None
                eng = [dve, act][cnt[0] % 2] if special is None else dve
                if special is not None:
                    special(eng, ti, to, g, ri, ro)
                    dw = special.out_cols
                    (pool if i % 2 else sp).dma(dst[kc * 128:(kc + 1) * 128, 0:dw], to[:, 0:dw], ro, r_w, load=False)
                    yield
                    continue
                if eng is act:
                    if g is not None:
                        eng.op(lambda e: e.activation(out=to[:, 0:w], in_=ti[:, 0:w], func=AF.Copy, scale=g),
                               reads=[ri, r_gvec, r_gsc], writes=[ro])
                    else:
                        eng.op(lambda e: e.activation(out=to[:, 0:w], in_=ti[:, 0:w], func=AF.Copy,
                                                      scale=float(const or 1.0)), reads=[ri], writes=[ro])
                else:
                    if g is not None:
                        eng.op(lambda e: e.tensor_scalar(out=to[:, 0:w], in0=ti[:, 0:w], scalar1=g, scalar2=None,
                                                         op0=ALU.mult), reads=[ri, r_gvec, r_gsc], writes=[ro])
                    else:
                        eng.op(lambda e: e.tensor_scalar(out=to[:, 0:w], in0=ti[:, 0:w],
                                                         scalar1=float(const or 1.0), scalar2=None, op0=ALU.mult),
                               reads=[ri], writes=[ro])
                q = pool if i % 2 else sp
                q.dma(dst[kc * 128:(kc + 1) * 128, c0:c1], to[:, 0:w], ro, r_w, load=False)
                yield

    def sp_qb(eng, ti, to, g, ri, ro):
        iv = ti[:, 0:768].rearrange("p (h c) -> p h c", c=96)
        ov = to[:, 0:1024].rearrange("p (h c) -> p h c", c=128)
        for (o0, o1, i0_, i1_) in ((0, 96, 0, 96), (96, 112, 80, 96), (112, 128, 64, 80)):
            eng.op(lambda e: e.tensor_scalar(out=ov[:, :, o0:o1], in0=iv[:, :, i0_:i1_], scalar1=g, scalar2=None,
                                             op0=ALU.mult), reads=[ri, r_gsc], writes=[ro])
    sp_qb.out_cols = 1024

    def sp_kvb(eng, ti, to, g, ri, ro):
        iv = ti[:, 0:1024].rearrange("p (h t d) -> p h t d", t=2, d=64)
        ov = to[:, 0:1024].rearrange("p (t h d) -> p t h d", t=2, d=64)
        for t in range(2):
            eng.op(lambda e: e.tensor_scalar(out=ov[:, t, :, :], in0=iv[:, :, t, :], scalar1=g, scalar2=None,
                                             op0=ALU.mult), reads=[ri, r_gvec], writes=[ro])
    sp_kvb.out_cols = 1024

    win_gain = lambda kc, ex: (gsc[:, kc:kc + 1] if ex and ex[0] else gvec[:, kc:kc + 1])

    def prep_rest(stg, cnt):
        yield from prep(stg, cnt, w_in, WIN, D, IN_COLS, gain=win_gain, blocks=[(0, 768), (2592, 4640)])
        yield from prep(stg, cnt, w_qb, WQB, 768, 768, gain=lambda kc, ex: gsc[:, 8 + kc:9 + kc], blocks=[(0, 768)],
                        special=sp_qb)
        yield from prep(stg, cnt, w_br_a, WBA, 512, D, const=1.0)
        yield from prep(stg, cnt, w_br_b, WBB, 512, D, const=1.0)
        yield from prep(stg, cnt, w_o, WO, D, D, const=0.5)
        yield from prep(stg, cnt, w_up, WUP, D, 2 * DFF, gain=lambda kc, ex: gvec[:, 16 + kc:17 + kc])
        yield from prep(stg, cnt, w_down, WDN, DFF, D, const=1.0)

    with ExitStack() as st0:
        stg0 = make_staging(st0, "a", 4)
        cnt0 = [0]
        for _ in prep(stg0, cnt0, w_in, WIN, D, IN_COLS, gain=win_gain, blocks=[(768, 1056), (1056, 1568, True), (1568, 2592)]):
            pass
        for _ in prep(stg0, cnt0, w_kvb, WKVB, 256, 1024, gain=lambda kc, ex: gvec[:, 14 + kc:15 + kc], blocks=[(0, 1024)],
                      special=sp_kvb):
            pass
        if debug and debug.get("stop") in (0, 1):
            for _ in prep_rest(stg0, cnt0):
                pass
        kb.barrier()

    if debug and debug.get("stop") == 0:
        kb.finish()
        return nc

    def rms_rstd(ss_ap, out_ap, n, r_ss, r_out, width):
        r_ss = r_ss if isinstance(r_ss, list) else [r_ss]
        r_out = r_out if isinstance(r_out, list) else [r_out]
        act.op(lambda e: e.activation(out=out_ap, in_=ss_ap, func=AF.Sqrt, bias=epsT[:, 0:1], scale=1.0 / n),
               reads=r_ss + [r_eps], writes=r_out)
        dve.op(lambda e: e.reciprocal(out=out_ap, in_=out_ap), reads=r_out, writes=r_out)

    with ExitStack() as st1:
        def sb1(name, shape, dt, dma=False):
            t = st1.enter_context(nc.sbuf_tensor(name, list(shape), dt))
            return t, kb.res(name, dma)

        wkv, r_wkv = sb1("p1_wkv", [128, 8, 288], BF16, dma=True)
        wna, r_wna = sb1("p1_wna", [128, 8, 1536], BF16, dma=True)
        wkvb, r_wkvb = sb1("p1_wkvb", [128, 2, 1024], BF16, dma=True)
        sp.dma(wkv[:], WIN[:, 768:1056].rearrange("(c p) n -> p c n", p=128), r_wkv, r_w)
        sp.dma(wna[:], WIN[:, 1056:2592].rearrange("(c p) n -> p c n", p=128), r_wna, r_w)
        sp.dma(wkvb[:], WKVB[:, :].rearrange("(c p) n -> p c n", p=128), r_wkvb, r_w)
        xt = [sb1("p1_x%d" % i, [128, 4, D], F32, dma=True) for i in range(2)]
        rt = [sb1("p1_rope%d" % i, [128, 4, 64], F32, dma=True) for i in range(2)]
        dbl = {}
        for nm_, shp_, dt_ in (("xn", [128, 4, D], BF16), ("junk", [128, D], BF16), ("ss", [128, 8], F32),
                               ("rstd", [128, 8], F32), ("hT", [128, 8, 512], BF16), ("ckv", [128, 4, 288], F32),
                               ("ckvn", [128, 4, 256], BF16), ("ckvT", [128, 2, 512], BF16), ("t1", [128, 4, 32], F32),
                               ("t2", [128, 4, 32], F32), ("kpe", [128, 4, 32], BF16)):
            dbl[nm_] = [sb1("p1_%s%d" % (nm_, i), shp_, dt_) for i in range(2)]
        fine = {k_: [[kb.res("p1f_%s%d_%d" % (k_, i, c_)) for c_ in range(n_)] for i in range(2)]
                for k_, n_ in (("xn", 4), ("hT", 8), ("ckv", 4), ("ckvn", 4), ("ckvT", 2), ("ss", 8), ("rstd", 8))}
        for i in range(2):
            dbl["junk"][i][1].scratch = True
        junk2, r_junk2 = sb1("p1_junk2", [128, D], BF16)
        r_junk2.scratch = True
        NST = 2
        kpT = [sb1("p1_kpT%d" % i, [32, 512], BF16, dma=True) for i in range(NST)]
        knT = [sb1("p1_knT%d" % i, [128, 4, 512], BF16, dma=True) for i in range(NST)]
        vst = [sb1("p1_vst%d" % i, [128, H, 4, 128], BF16, dma=True) for i in range(NST)]
        nqT = [sb1("p1_nqT%d" % i, [128, 4, 512], BF16, dma=True) for i in range(NST)]
        nkT = [sb1("p1_nkT%d" % i, [128, 4, 512], BF16, dma=True) for i in range(NST)]
        nvs = [sb1("p1_nvs%d" % i, [128, 4, H, 128], BF16, dma=True) for i in range(NST)]
        for i in range(NST):
            dve.op(lambda e: e.memset(vst[i][0][:, :, :, 64:128], 1.0), writes=[vst[i][1]])
            dve.op(lambda e: e.memset(nvs[i][0][:, :, :, 64:128], 1.0), writes=[nvs[i][1]])
        bank_rr = [0]

        def nb():
            bank_rr[0] = (bank_rr[0] + 1) % 8
            return bank_rr[0]

        evac_rr = [0]

        def evac(out_ap, in_ap, r_out, r_in):
            evac_rr[0] += 1
            if evac_rr[0] % 2:
                act.op(lambda e: e.activation(out=out_ap, in_=in_ap, func=AF.Copy), reads=[r_in], writes=[r_out])
            else:
                dve.op(lambda e: e.tensor_copy(out=out_ap, in_=in_ap), reads=[r_in], writes=[r_out])

        def phase1_tile(x_rows, rope_rows, seq, col0, do_mla, do_na, ti):
            (xn, r_xn), (junk, r_junk), (ss, r_ss), (rstd, r_rstd), (hT, r_hT), (ckv, r_ckv), (ckvn, r_ckvn), \
                (ckvT, r_ckvT), (t1, r_t1), (t2, r_t2), (kpe, r_kpe) = [dbl[k_][ti % 2] for k_ in (
                    "xn", "junk", "ss", "rstd", "hT", "ckv", "ckvn", "ckvT", "t1", "t2", "kpe")]
            r_xn, r_hT, r_ckv, r_ckvn, r_ckvT, r_ss, r_rstd = [fine[k_][ti % 2] for k_ in (
                "xn", "hT", "ckv", "ckvn", "ckvT", "ss", "rstd")]
            X, rX = xt[ti % 2]
            sp.dma(X[:], x_rows.rearrange("(j p) d -> p j d", p=128), rX)
            if do_mla:
                RT, rRT = rt[ti % 2]
                sp.dma(RT[:], rope_rows.rearrange("(j p) d -> p j d", p=128), rRT)
            for j in range(4):
                if j % 2:
                    act.op(lambda e: e.activation(out=junk[:], in_=X[:, j, :], func=AF.Square, accum_out=ss[:, j:j + 1]),
                           reads=[rX], writes=[r_junk, r_ss[j]])
                else:
                    dve.op(lambda e: e.scalar_tensor_tensor(out=junk2[:], in0=X[:, j, :], scalar=1.0, in1=X[:, j, :],
                                                            op0=ALU.mult, op1=ALU.mult, accum_out=ss[:, j:j + 1]),
                           reads=[rX], writes=[r_junk2, r_ss[j]])
            rms_rstd(ss[:, 0:4], rstd[:, 0:4], D, r_ss[0:4], r_rstd[0:4], 4)
            for j in range(4):
                kb.scale_cast(xn[:, j, :], X[:, j, :], rstd[:, j:j + 1], [rX, r_rstd[j]], [r_xn[j]])
            yield
            for f in range(8):
                b = nb()
                pt, rpt = PB(b)
                ptb = PBbf(b)
                for j in range(4):
                    pe.op(lambda e: e.transpose(out=ptb[:, j * 128:(j + 1) * 128], in_=xn[:, j, f * 128:(f + 1) * 128],
                                                identity=ident[:]), reads=[r_xn[j], r_ident], writes=[rpt], inc=(j == 3))
                evac(hT[:, f, :], ptb[:, 0:512], r_hT[f], rpt)
            yield
            s = ti % NST
            if do_mla:
                for j in range(4):
                    b = nb()
                    pt, rpt = PB(b)
                    for f in range(8):
                        pe.op(lambda e: e.matmul(pt[:, 0:288], lhsT=hT[:, f, j * 128:(j + 1) * 128], rhs=wkv[:, f, :],
                                                 start=(f == 0), stop=(f == 7)), reads=[r_hT[f], r_wkv], writes=[rpt],
                              inc=(f == 7))
                    act.op(lambda e: e.activation(out=ckv[:, j, :], in_=pt[:, 0:288], func=AF.Copy), reads=[rpt],
                           writes=[r_ckv[j]])
                    dve.op(lambda e: e.scalar_tensor_tensor(out=junk[:, 0:256], in0=ckv[:, j, 0:256], scalar=1.0,
                                                            in1=ckv[:, j, 0:256], op0=ALU.mult, op1=ALU.mult,
                                                            accum_out=ss[:, 4 + j:5 + j]),
                           reads=[r_ckv[j]], writes=[r_junk, r_ss[4 + j]])
                rms_rstd(ss[:, 4:8], rstd[:, 4:8], 256, r_ss[4:8], r_rstd[4:8], 4)
                for j in range(4):
                    kb.scale_cast(ckvn[:, j, :], ckv[:, j, 0:256], rstd[:, 4 + j:5 + j], [r_ckv[j], r_rstd[4 + j]],
                                  [r_ckvn[j]])
                dve.op(lambda e: e.tensor_tensor(out=t1[:], in0=ckv[:, :, 256:288], in1=RT[:, :, 0:32], op=ALU.mult),
                       reads=r_ckv + [rRT], writes=[r_t1])
                dve.op(lambda e: e.tensor_tensor(out=t2[:, :, 0:16], in0=ckv[:, :, 272:288], in1=RT[:, :, 32:48],
                                                 op=ALU.mult), reads=r_ckv + [rRT], writes=[r_t2])
                dve.op(lambda e: e.tensor_tensor(out=t2[:, :, 16:32], in0=ckv[:, :, 256:272], in1=RT[:, :, 48:64],
                                                 op=ALU.mult), reads=r_ckv + [rRT], writes=[r_t2])
                dve.op(lambda e: e.tensor_tensor(out=kpe[:], in0=t1[:], in1=t2[:], op=ALU.add), reads=[r_t1, r_t2],
                       writes=[r_kpe])
            yield
            if do_mla:
                for kc in range(2):
                    b = nb()
                    pt, rpt = PB(b)
                    ptb = PBbf(b)
                    for j in range(4):
                        pe.op(lambda e: e.transpose(out=ptb[:, j * 128:(j + 1) * 128],
                                                    in_=ckvn[:, j, kc * 128:(kc + 1) * 128], identity=ident[:]),
                              reads=[r_ckvn[j], r_ident], writes=[rpt], inc=(j == 3))
                    evac(ckvT[:, kc, :], ptb[:, 0:512], r_ckvT[kc], rpt)
                b = nb()
                pt, rpt = PB(b)
                ptb = PBbf(b)
                KPT, rKPT = kpT[s]
                for j in range(4):
                    pe.op(lambda e: e.transpose(out=ptb[0:32, j * 128:(j + 1) * 128], in_=kpe[:, j, :], identity=ident[:]),
                          reads=[r_kpe, r_ident], writes=[rpt], inc=(j == 3))
                evac(KPT[:, :], ptb[0:32, 0:512], rKPT, rpt)
                pool.dma(seq.KP[:, col0:col0 + 512], KPT[:, :], rKPT, seq.r_kv, load=False)
            yield
            if do_mla:
                KNT, rKNT = knT[s]
                for hp in range(4):
                    b = nb()
                    pt, rpt = PB(b)
                    for kc in range(2):
                        pe.op(lambda e: e.matmul(pt[:, :], lhsT=wkvb[:, kc, hp * 128:(hp + 1) * 128], rhs=ckvT[:, kc, :],
                                                 start=(kc == 0), stop=(kc == 1)), reads=[r_wkvb, r_ckvT[kc]], writes=[rpt],
                              inc=(kc == 1))
                    evac(KNT[:, hp, :], pt[:, :], rKNT, rpt)
                pool.dma(seq.KN[:, col0:col0 + 512].rearrange("(c p) n -> p c n", p=128), KNT[:], rKNT, seq.r_kv,
                         load=False)
                VS, rVS = vst[s]
                for j in range(4):
                    b = nb()
                    pt, rpt = PB(b)
                    for kc in range(2):
                        pe.op(lambda e: e.matmul(pt[:, :], lhsT=ckvT[:, kc, j * 128:(j + 1) * 128], rhs=wkvb[:, kc, 512:1024],
                                                 start=(kc == 0), stop=(kc == 1)), reads=[r_wkvb, r_ckvT[kc]], writes=[rpt],
                              inc=(kc == 1))
                    evac(VS[:, :, j, 0:64], pt[:, :].rearrange("p (h d) -> p h d", d=64), rVS, rpt)
                blk0 = col0 // 128
                pool.dma(seq.VD[:, :, blk0:blk0 + 4, :].rearrange("h p j d -> p h j d"), VS[:], rVS, seq.r_kv, load=False)
            if do_na:
                for (dst_t, c_off, dram) in ((nqT[s], 0, seq.NQ), (nkT[s], 512, seq.NK)):
                    T_, rT_ = dst_t
                    for c4 in range(4):
                        b = nb()
                        pt, rpt = PB(b)
                        for f in range(8):
                            pe.op(lambda e: e.matmul(pt[:, :], lhsT=wna[:, f, c_off + c4 * 128:c_off + (c4 + 1) * 128],
                                                     rhs=hT[:, f, :], start=(f == 0), stop=(f == 7)),
                                  reads=[r_wna, r_hT[f]], writes=[rpt], inc=(f == 7))
                        evac(T_[:, c4, :], pt[:, :], rT_, rpt)
                    pool.dma(dram[:, col0:col0 + 512].rearrange("(c p) n -> p c n", p=128), T_[:], rT_, seq.r_na,
                             load=False)
                NVS, rNVS = nvs[s]
                for j in range(4):
                    b = nb()
                    pt, rpt = PB(b)
                    for f in range(8):
                        pe.op(lambda e: e.matmul(pt[:, :], lhsT=hT[:, f, j * 128:(j + 1) * 128], rhs=wna[:, f, 1024:1536],
                                                 start=(f == 0), stop=(f == 7)), reads=[r_wna, r_hT[f]], writes=[rpt],
                              inc=(f == 7))
                    evac(NVS[:, j, :, 0:64], pt[:, :].rearrange("p (h d) -> p h d", d=64), rNVS, rpt)
                pool.dma(seq.NV[col0:col0 + 512, :].rearrange("(j p) c -> p j c", p=128),
                         NVS[:].rearrange("p j h d -> p j (h d)"), rNVS, seq.r_na,
                         load=False)

        p1_lim = debug.get("p1_tiles") if debug else None
        jobs = []
        for si, nm in enumerate(("p0", "p1")):
            for t in range(SP_LEN // 512 if p1_lim is None else p1_lim):
                jobs.append((xp[si, t * 512:(t + 1) * 512, :], ropek[t * 512:(t + 1) * 512, :], seqs[nm], t * 512, True, True))
        if not (debug and debug.get("skip_sample")):
            for t in range(SS_LEN // 512 if p1_lim is None else p1_lim):
                jobs.append((xs[t * 512:(t + 1) * 512, :], ropek[t * 512:(t + 1) * 512, :], seqs["sm"], t * 512, True, False))
            for t in range(XO_LEN // 512 if p1_lim is None else p1_lim):
                jobs.append((xo[t * 512:(t + 1) * 512, :], None, seqs["so"], t * 512, False, True))
        gens = [phase1_tile(*job, ti) for ti, job in enumerate(jobs)]
        next(gens[0])
        if len(gens) > 1:
            next(gens[1])
        next(gens[0])
        next(gens[0])
        for ti in range(len(gens)):
            if ti + 2 < len(gens):
                next(gens[ti + 2])
            if ti + 1 < len(gens):
                next(gens[ti + 1])
            next(gens[ti])
            if ti + 1 < len(gens):
                next(gens[ti + 1])
            for _ in gens[ti]:
                pass
        kb.barrier()

    if debug and debug.get("stop") == 1:
        kb.finish()
        return nc

    with ExitStack() as st2:
        def sb2(name, shape, dt, dma=False):
            t = st2.enter_context(nc.sbuf_tensor(name, list(shape), dt))
            return t, kb.res(name, dma)

        ntab_max = max(n_ptab, n_stab)
        tab, r_tab = sb2("nb_tab", [128, ntab_max, H, 256], BF16, dma=True)

        def load_tabs(kind):
            src, n_ = (ptab, n_ptab) if kind == "p" else (stab, n_stab)
            for t0_ in range(0, n_, 4):
                t1_ = min(t0_ + 4, n_)
                sp.dma(tab[:, t0_:t1_, :, :], src[t0_:t1_].rearrange("t p h q -> p t h q"), r_tab)

        nq = [[sb2("nb_q%d_%d" % (i, par_), [128, 4, 512], BF16, dma=True) for par_ in range(2)] for i in range(2)]
        for i in range(2):
            dve.op(lambda e: e.memset(nq[i][0][0][64:128, :, :], 0.0), writes=[nq[i][0][1]])
            dve.op(lambda e: e.memset(nq[i][1][0][0:64, :, :], 0.0), writes=[nq[i][1][1]])
        nk = [sb2("nb_k%d" % i, [128, 4, 1024], BF16, dma=True) for i in range(2)]
        nv = [sb2("nb_v%d" % i, [128, 8, 1024], BF16, dma=True) for i in range(2)]
        npt = [sb2("nb_pt%d" % i, [128, 1536], BF16) for i in range(2)]
        nob = [sb2("nb_ob%d" % i, [128, 4, 512], BF16, dma=True) for i in range(2)]
        nrec, r_nrec = sb2("nb_rec", [128, 256], F32)
        itc = [0]

        def na_tile(kind, seq, rows, ti, tindex):
            off = 0 if kind == "p" else FOFF
            r0 = rows[0]
            qw = len(rows) * 64
            groups = [rows[i:i + 4] for i in range(0, len(rows), 4)]
            gblks = [na_group_blocks(kind, g_) for g_ in groups]
            a_lo = min(min(b) for b in gblks)
            a_hi = max(max(b) for b in gblks)
            nrow_k = a_hi + 2 - a_lo
            (Qe, rQe), (Qo, rQo) = nq[ti % 2]
            Kt, rK = nk[ti % 2]
            V, rV = nv[ti % 2]
            OBt, rOB = nob[ti % 2]
            qc0 = (r0 + off) * 64
            nq_v = seq.NQ[:, qc0:qc0 + qw].rearrange("(c p) n -> p c n", p=128)
            sp.dma(Qe[0:64, :, 0:qw], nq_v[0:64], rQe, seq.r_na)
            sp.dma(Qo[64:128, :, 0:qw], nq_v[64:128], rQo, seq.r_na)
            sp.dma(Kt[:, :, 0:nrow_k * 64], seq.NK[:, a_lo * 64:(a_hi + 2) * 64].rearrange("(c p) n -> p c n", p=128),
                   rK, seq.r_na)
            sp.dma(V[:, 0:nrow_k // 2, :], seq.NV[a_lo * 64:(a_hi + 2) * 64, :].rearrange("(b p) c -> p b c", p=128),
                   rV, seq.r_na)
            its = [(h, gi) for h in range(H) for gi in range(len(groups))]
            base = itc[0]
            itc[0] += len(its)

            def qkb(k_):
                h, gi = its[k_]
                hp, hc = (h % 2) * 64, h // 2
                g_, blks = groups[gi], gblks[gi]
                par = (base + k_) % 2
                PT, rPT = npt[par]
                nbank = (len(blks) + 1) // 2
                assert nbank <= 3
                sb = [2 + par * 3 + i for i in range(nbank)]
                for bi in range(nbank):
                    pt, rpt = PB(sb[bi])
                    bb = blks[bi * 2:bi * 2 + 2]
                    for ej, a in enumerate(bb):
                        Qm, rQm = (Qe, rQe) if h % 2 == 0 else (Qo, rQo)
                        pe.op(lambda e: e.matmul(pt[:, ej * 256:(ej + 1) * 256],
                                                 lhsT=Kt[:, hc, (a - a_lo) * 64:(a - a_lo) * 64 + 128],
                                                 rhs=Qm[:, hc, gi * 256:(gi + 1) * 256], start=(ej == 0),
                                                 stop=False, skip_group_check=True),
                              reads=[rK, rQm], writes=[rpt], inc=False)
                    for ej, a in enumerate(bb):
                        tid = tindex[(g_[0], a)]
                        pe.op(lambda e: e.matmul(pt[:, ej * 256:(ej + 1) * 256], lhsT=ident[:, :], rhs=tab[:, tid, h, :],
                                                 start=False, stop=True, skip_group_check=True),
                              reads=[r_ident, r_tab], writes=[rpt], inc=(ej == len(bb) - 1))
                    w = len(bb) * 256
                    act.op(lambda e: e.activation(out=PT[:, bi * 512:bi * 512 + w], in_=pt[:, 0:w], func=AF.Exp),
                           reads=[rpt], writes=[rPT])

            def pv(k_):
                h, gi = its[k_]
                hp, hc = (h % 2) * 64, h // 2
                blks = gblks[gi]
                par = (base + k_) % 2
                PT, rPT = npt[par]
                ob_, rob_ = PB(par)
                for bi, a in enumerate(blks):
                    pe.op(lambda e: e.matmul(ob_[:, 0:256], lhsT=V[:, (a - a_lo) // 2, h * 128:(h + 1) * 128],
                                             rhs=PT[:, bi * 256:(bi + 1) * 256], start=(bi == 0),
                                             stop=(bi == len(blks) - 1)),
                          reads=[rV, rPT], writes=[rob_], inc=(bi == len(blks) - 1))
                dve.op(lambda e: e.reciprocal(out=nrec[64:128, :], in_=ob_[64:128, 0:256]), reads=[rob_],
                       writes=[r_nrec])
                dve.op(lambda e: e.tensor_tensor(out=OBt[hp:hp + 64, hc, gi * 256:(gi + 1) * 256], in0=ob_[0:64, 0:256],
                                                 in1=nrec[64:128, :], op=ALU.mult), reads=[rob_, r_nrec], writes=[rOB])

            qkb(0)
            for k_ in range(len(its)):
                if k_ + 1 < len(its):
                    qkb(k_ + 1)
                pv(k_)
            pool.dma(seq.OB[:, qc0:qc0 + qw].rearrange("(c p) n -> p c n", p=128), OBt[:, :, 0:qw], rOB, seq.r_ob, load=False)

        ti = 0
        nb_lim = debug.get("nb_tiles") if debug else None
        stg1 = make_staging(st2, "b", 4)
        prest = prep_rest(stg1, [0])

        def advance(n):
            for _ in range(n):
                if next(prest, "done") == "done":
                    break

        load_tabs("p")
        for nm in ("p0", "p1"):
            for t in range(8 if nb_lim is None else nb_lim):
                na_tile("p", seqs[nm], list(range(t * 8, t * 8 + 8)), ti, ptab_index)
                advance(5)
                ti += 1
        if not (debug and debug.get("skip_sample")):
            load_tabs("s")
            for t in range(4 if nb_lim is None else nb_lim):
                na_tile("s", seqs["so"], list(range(t * 8, t * 8 + 8)), ti, stab_index)
                advance(5)
                ti += 1
            for rr in (-4, OWN // GRID_W):
                na_tile("s", seqs["so"], list(range(rr, rr + 4)), ti, stab_index)
                ti += 1
        advance(10000)
        kb.barrier()

    if debug and debug.get("stop") == 2:
        kb.finish()
        return nc

    with ExitStack() as st3:
        def sb3(name, shape, dt, dma=False):
            t = st3.enter_context(nc.sbuf_tensor(name, list(shape), dt))
            return t, kb.res(name, dma)

        NR = 4
        ring = [sb3("w_ring%d" % i, [128, 8192], BF16, dma=True) for i in range(NR)]
        NX = 2
        Xb = []
        for i in range(NX):
            t_, _ = sb3("w_x%d" % i, [128, 4, D], F32)
            Xb.append((t_, [kb.res("w_x%d_%d" % (i, j), dma=True) for j in range(4)]))
        Xst = [[kb.res("w_xst%d_%d" % (i, j), dma=True) for j in range(4)] for i in range(NX)]
        xn, _ = sb3("w_xn", [128, 4, D], BF16)
        r_xn = [kb.res("w_xn%d" % j) for j in range(4)]
        hT, _ = sb3("w_hT", [128, 8, 512], BF16)
        r_hT = [kb.res("w_hT%d" % j) for j in range(4)]
        big, _ = sb3("w_big", [128, NFF * 512], BF16)
        aT = big[:, :].rearrange("p (c n) -> p c n", n=512)
        r_aT = kb.res("w_aT")
        cqn = big[:, 0:4 * 768].rearrange("p (j n) -> p j n", n=768)
        r_cqn = kb.res("w_cqn")
        cqnT = big[:, 3072:3072 + 6 * 512].rearrange("p (c n) -> p c n", n=512)
        r_cqnT = kb.res("w_cqnT")
        qsb = big[:, 6144:6144 + 8 * 512].rearrange("p (h n) -> p h n", n=512)
        r_qsb = kb.res("w_qsb")
        r_aT.alias = [r_cqn, r_cqnT, r_qsb]
        for r_ in (r_cqn, r_cqnT, r_qsb):
            r_.alias = [r_aT]
        rq = [sb3("w_rq%d" % i, [64, 512], F32, dma=True) for i in range(2)]
        oaT, r_oaT = sb3("w_oaT", [128, 4, 512], BF16)
        obT = [sb3("w_obT%d" % i, [128, 4, 512], BF16, dma=True) for i in range(1)]
        mT, r_mT = sb3("w_mT", [128, 8, 512], BF16)
        NKV = 4
        kbuf = [sb3("w_kb%d" % i, [128, KC], BF16, dma=True) for i in range(NKV)]
        vbuf = [sb3("w_vb%d" % i, [128, NBLK, 128], BF16, dma=True) for i in range(NKV)]
        rec, r_rec = sb3("w_rec", [128, 512], F32)
        tg = [sb3("w_tg%d" % i, [128, 512], F32) for i in range(2)]
        tm = [sb3("w_tm%d" % i, [128, 512], F32) for i in range(2)]
        tcv = [sb3("w_tc%d" % i, [128, 512], F32) for i in range(2)]
        th = [sb3("w_th%d" % i, [128, 512], F32) for i in range(2)]

        bank_rr = [0]

        def nb():
            bank_rr[0] = (bank_rr[0] + 1) % 8
            return bank_rr[0]

        evac_rr = [0]

        def evac(out_ap, in_ap, r_out, r_in):
            evac_rr[0] += 1
            if evac_rr[0] % 2:
                act.op(lambda e: e.activation(out=out_ap, in_=in_ap, func=AF.Copy), reads=[r_in], writes=[r_out])
            else:
                dve.op(lambda e: e.tensor_copy(out=out_ap, in_=in_ap), reads=[r_in], writes=[r_out])

        def piece_defs():
            d = [("cq", [(WIN[:, 0:768], 8, 768)]),
                 ("qb", [(WQB[:, :], 6, 1024)]),
                 ("br", [(WBA[:, :], 4, 1024), (WBB[:, :], 4, 1024)])]
            for g in range(2):
                d.append(("g%d" % g, [(WIN[:, 2592 + g * 512:2592 + (g + 1) * 512], 8, 512),
                                      (WIN[:, 3616 + g * 512:3616 + (g + 1) * 512], 8, 512)]))
            d.append(("o", [(WO[:, :], 8, 1024)]))
            for g in range(6):
                w = 512 if g < 5 else 256
                d.append(("up%d" % g, [(WUP[:, g * 512:g * 512 + w], 8, w), (WUP[:, DFF + g * 512:DFF + g * 512 + w], 8, w)]))
            for g in range(3):
                k0, k1 = g * 8, min(g * 8 + 8, NFF)
                d.append(("dn%d" % g, [(WDN[k0 * 128:k1 * 128, :], k1 - k0, 1024)]))
            return d

        PDEF = piece_defs()
        NPIECE = len(PDEF)
        pstate = {"issued": 0, "total": 0, "released": 0}

        def release_piece(win_i, name):
            li = [n for n, _ in PDEF].index(name)
            gidx = win_i * NPIECE + li
            pstate["released"] = max(pstate["released"], gidx + 1)

        def issue_piece(gidx):
            name, parts = PDEF[gidx % NPIECE]
            t, r = ring[gidx % NR]
            o = 0
            for (src, kcn, w) in parts:
                sp.dma(t[:, o:o + kcn * w].rearrange("p (c n) -> p c n", n=w), src.rearrange("(c p) n -> p c n", p=128), r, r_w)
                o += kcn * w

        def get_piece(win_i, name, ahead=2):
            li = [n for n, _ in PDEF].index(name)
            gidx = win_i * NPIECE + li
            while (pstate["issued"] <= min(gidx + ahead, pstate["total"] - 1)
                   and pstate["issued"] - NR < pstate["released"]):
                issue_piece(pstate["issued"])
                pstate["issued"] += 1
            assert pstate["issued"] > gidx, (name, gidx, pstate)
            t, r = ring[gidx % NR]
            views, o = [], 0
            for (src, kcn, w) in PDEF[li][1]:
                views.append(t[:, o:o + kcn * w].rearrange("p (c n) -> p c n", n=w))
                o += kcn * w
            return views, r

        def rstd_of(ss_ap, out_ap, n, np_=128):
            act.op(lambda e: e.activation(out=out_ap, in_=ss_ap, func=AF.Sqrt, bias=epsT[0:np_, 0:1], scale=1.0 / n),
                   reads=[r_ss, r_eps], writes=[r_rstd])
            dve.op(lambda e: e.reciprocal(out=out_ap, in_=out_ap), reads=[r_rstd], writes=[r_rstd])

        P2 = [sb3("w_p2_%d" % i, [128, 2, 512], BF16) for i in range(3)]
        sst = {k_: (sb3("w_ss_" + k_, [128, 8], F32)[0], [kb.res("w_ss_%s%d" % (k_, c_)) for c_ in range(8)])
               for k_ in ("h", "q", "n2", "f")}
        rst = {k_: (sb3("w_rs_" + k_, [128, 8], F32)[0], [kb.res("w_rs_%s%d" % (k_, c_)) for c_ in range(8)])
               for k_ in ("h", "q", "n2", "f")}
        junks = [sb3("w_junk%d" % i, [128, D], BF16) for i in range(3)]
        jrr = [0]

        def jk():
            jrr[0] += 1
            return junks[jrr[0] % 3]

        def rstd2(tag, col, np_, n):
            s_, rs_ = sst[tag]
            o_, ro_ = rst[tag]
            act.op(lambda e: e.activation(out=o_[0:np_, col:col + 1], in_=s_[0:np_, col:col + 1], func=AF.Sqrt,
                                          bias=epsT[0:np_, 0:1], scale=1.0 / n), reads=[rs_[col], r_eps], writes=[ro_[col]])
            dve.op(lambda e: e.reciprocal(out=o_[0:np_, col:col + 1], in_=o_[0:np_, col:col + 1]), reads=[ro_[col]],
                   writes=[ro_[col]])

        def norm_a(st, tag):
            X, rX = st["X"]
            s_, rs_ = sst[tag]
            o_, ro_ = rst[tag]
            for j, sz in enumerate(st["subs"]):
                jt, rj = jk()
                act.op(lambda e: e.activation(out=jt[0:sz, :], in_=X[0:sz, j, :], func=AF.Square,
                                              accum_out=s_[0:sz, j:j + 1]), reads=[rX[j]], writes=[rj, rs_[j]])
                rstd2(tag, j, sz, D)
                kb.scale_cast(xn[0:sz, j, :], X[0:sz, j, :], o_[0:sz, j:j + 1], [rX[j], ro_[j]], [r_xn[j]])

        def norm_b(st):
            for j, sz in enumerate(st["subs"]):
                b = nb()
                pt, rpt = PB(b)
                ptb = PBbf(b)
                for f in range(8):
                    pe.op(lambda e: e.transpose(out=ptb[:, f * 128:f * 128 + sz], in_=xn[0:sz, j, f * 128:(f + 1) * 128],
                                                identity=ident[0:sz, 0:sz]), reads=[r_xn[j], r_ident], writes=[rpt],
                          inc=(f == 7))
                evac(hT[:, :, j * 128:j * 128 + sz], ptb.rearrange("p (f n) -> p f n", n=128)[:, :, 0:sz], r_hT[j], rpt)

        kvstate = {"n": 0}

        def w_state(wi, W):
            ncols = W["ncols"]
            st = dict(W)
            st["wi"] = wi
            st["subs"] = [min(128, ncols - j * 128) for j in range((ncols + 127) // 128)]
            st["X"] = Xb[wi % NX]
            st["RQ"] = rq[wi % 2]
            return st

        def s_load(st):
            ncols = st["ncols"]
            X, rX = st["X"]
            for j, sz in enumerate(st["subs"]):
                sp.dma(X[0:sz, j, :], st["x"][j * 128:j * 128 + sz, :], rX[j])
            RQ, rRQ = st["RQ"]
            sp.dma(RQ[:, 0:ncols], st["ropeq"], rRQ)

        def s_cq(st):
            wi, ncols, subs = st["wi"], st["ncols"], st["subs"]
            OBT, rOBT = obT[0]
            sp.dma(OBT[:, :, 0:ncols], st["ob"].rearrange("(c p) n -> p c n", p=128), rOBT, st["r_ob"])
            (wcq,), r_wcq = get_piece(wi, "cq")
            s_, rs_ = sst["q"]
            o_, ro_ = rst["q"]
            for j, sz in enumerate(subs):
                bks = []
                for half in range(2):
                    b = nb()
                    pt, rpt = PB(b)
                    bks.append((pt, rpt))
                    for f in range(8):
                        pe.op(lambda e: e.matmul(pt[0:sz, 0:384], lhsT=hT[:, f, j * 128:j * 128 + sz],
                                                 rhs=wcq[:, f, half * 384:(half + 1) * 384], start=(f == 0), stop=(f == 7)),
                              reads=[r_hT[j], r_wcq], writes=[rpt], inc=(f == 7))
                    jt, rj = jk()
                    act.op(lambda e: e.activation(out=jt[0:sz, 0:384], in_=pt[0:sz, 0:384], func=AF.Square,
                                                  accum_out=s_[0:sz, half:half + 1]), reads=[rpt], writes=[rj, rs_[half]])
                dve.op(lambda e: e.tensor_tensor(out=s_[0:sz, 2:3], in0=s_[0:sz, 0:1], in1=s_[0:sz, 1:2], op=ALU.add),
                       reads=[rs_[0], rs_[1]], writes=[rs_[2]])
                rstd2("q", 2, sz, 768)
                for half in range(2):
                    pt, rpt = bks[half]
                    if half == 0:
                        act.op(lambda e: e.activation(out=cqn[0:sz, j, 0:384], in_=pt[0:sz, 0:384], func=AF.Copy,
                                                      scale=o_[0:sz, 2:3]), reads=[rpt, ro_[2]], writes=[r_cqn])
                    else:
                        dve.op(lambda e: e.tensor_scalar(out=cqn[0:sz, j, 384:768], in0=pt[0:sz, 0:384],
                                                         scalar1=o_[0:sz, 2:3], scalar2=None, op0=ALU.mult),
                               reads=[rpt, ro_[2]], writes=[r_cqn])
            for kc in range(6):
                b = nb()
                pt, rpt = PB(b)
                ptb = PBbf(b)
                for j, sz in enumerate(subs):
                    pe.op(lambda e: e.transpose(out=ptb[:, j * 128:j * 128 + sz], in_=cqn[0:sz, j, kc * 128:(kc + 1) * 128],
                                                identity=ident[0:sz, 0:sz]), reads=[r_cqn, r_ident], writes=[rpt],
                          inc=(j == len(subs) - 1))
                evac(cqnT[:, kc, 0:ncols], ptb[:, 0:ncols], r_cqnT, rpt)
            release_piece(wi, "cq")

        def s_q(st):
            wi, ncols = st["wi"], st["ncols"]
            RQ, rRQ = st["RQ"]
            (wqb,), r_wqb = get_piece(wi, "qb")
            for h in range(H):
                bm = nb()
                pm, rpm = PB(bm)
                for kc in range(6):
                    pe.op(lambda e: e.matmul(pm[0:96, 0:ncols], lhsT=wqb[:, kc, h * 128:h * 128 + 96], rhs=cqnT[:, kc, 0:ncols],
                                             start=(kc == 0), stop=(kc == 5)), reads=[r_wqb, r_cqnT], writes=[rpm],
                          inc=(kc == 5))
                bs = nb()
                psw, rps = PB(bs)
                for kc in range(6):
                    pe.op(lambda e: e.matmul(psw[0:32, 0:ncols], lhsT=wqb[:, kc, h * 128 + 96:h * 128 + 128],
                                             rhs=cqnT[:, kc, 0:ncols], start=(kc == 0), stop=(kc == 5)),
                          reads=[r_wqb, r_cqnT], writes=[rps], inc=(kc == 5))
                act.op(lambda e: e.activation(out=qsb[0:64, h, 0:ncols], in_=pm[0:64, 0:ncols], func=AF.Copy),
                       reads=[rpm], writes=[r_qsb])
                t1, r_t1 = tm[0]
                t2, r_t2 = tm[1]
                dve.op(lambda e: e.tensor_tensor(out=t1[0:32, 0:ncols], in0=pm[64:96, 0:ncols], in1=RQ[0:32, 0:ncols],
                                                 op=ALU.mult), reads=[rpm, rRQ], writes=[r_t1])
                dve.op(lambda e: e.tensor_tensor(out=t2[0:32, 0:ncols], in0=psw[0:32, 0:ncols], in1=RQ[32:64, 0:ncols],
                                                 op=ALU.mult), reads=[rps, rRQ], writes=[r_t2])
                dve.op(lambda e: e.tensor_tensor(out=qsb[64:96, h, 0:ncols], in0=t1[0:32, 0:ncols], in1=t2[0:32, 0:ncols],
                                                 op=ALU.add), reads=[r_t1, r_t2], writes=[r_qsb])
            release_piece(wi, "qb")

        def s_attn(st):
            ncols = st["ncols"]
            kv = st["kv"]
            nch = st["S_kv"] // KC
            units = [(h, ch) for h in range(H) for ch in range(nch)]

            def load_chunk(ui):
                h, ch = units[ui]
                s_ = (kvstate["n"] + ui) % NKV
                KBt, rKB = kbuf[s_]
                VBt, rVB = vbuf[s_]
                sp.dma(KBt[0:64, :], kv.KN[h * 64:(h + 1) * 64, ch * KC:(ch + 1) * KC], rKB, kv.r_kv)
                sp.dma(KBt[64:96, :], kv.KP[:, ch * KC:(ch + 1) * KC], rKB, kv.r_kv)
                pool.dma(VBt[:, :, :], kv.VD[h, :, ch * NBLK:(ch + 1) * NBLK, :], rVB, kv.r_kv)

            nload = 0
            for ui in range(min(NKV - 1, len(units))):
                load_chunk(ui)
                nload += 1
            NPB = NBLK // 2
            pairs = [(ui, pb_) for ui in range(len(units)) for pb_ in range(NPB)]

            def emit_qk(i):
                ui, pb_ = pairs[i]
                h, ch = units[ui]
                KBt, rKB = kbuf[(kvstate["n"] + ui) % NKV]
                g = 1 + i % 3
                for t_ in range(2):
                    pt, rpt = PB(2 * g + t_)
                    b_ = 2 * pb_ + t_
                    pe.op(lambda e: e.matmul(pt[:, 0:ncols], lhsT=KBt[0:96, b_ * 128:(b_ + 1) * 128],
                                             rhs=qsb[0:96, h, 0:ncols], start=True, stop=True), reads=[rKB, r_qsb],
                          writes=[rpt], inc=(t_ == 1))

            emit_qk(0)
            emit_qk(1)
            for i, (ui, pb_) in enumerate(pairs):
                h, ch = units[ui]
                if pb_ == 0 and nload < len(units) and nload <= ui + NKV - 2:
                    load_chunk(nload)
                    nload += 1
                if i + 2 < len(pairs):
                    emit_qk(i + 2)
                VBt, rVB = vbuf[(kvstate["n"] + ui) % NKV]
                g = 1 + i % 3
                P_, rP = P2[i % 3]
                sview = pdbl[g][:, :].rearrange("p (b n) -> p b n", b=2)[:, :, 0:ncols]
                act.op(lambda e: e.activation(out=P_[:, :, 0:ncols], in_=sview, func=AF.Exp),
                       reads=[PB(2 * g)[1], PB(2 * g + 1)[1]], writes=[rP])
                ob_, rob_ = PB(h % 2)
                for t_ in range(2):
                    b_ = 2 * pb_ + t_
                    first = (ch == 0 and b_ == 0)
                    lastb = (ch == nch - 1 and b_ == NBLK - 1)
                    pe.op(lambda e: e.matmul(ob_[:, 0:ncols], lhsT=VBt[:, b_, :], rhs=P_[:, t_, 0:ncols], start=first,
                                             stop=lastb), reads=[rVB, rP], writes=[rob_], inc=(t_ == 1))
                if ch == nch - 1 and pb_ == NPB - 1:
                    dve.op(lambda e: e.reciprocal(out=rec[64:128, 0:ncols], in_=ob_[64:128, 0:ncols]), reads=[rob_],
                           writes=[r_rec])
                    hp, hc = (h % 2) * 64, h // 2
                    dve.op(lambda e: e.tensor_tensor(out=oaT[hp:hp + 64, hc, 0:ncols], in0=ob_[0:64, 0:ncols],
                                                     in1=rec[64:128, 0:ncols], op=ALU.mult), reads=[rob_, r_rec],
                           writes=[r_oaT])
            kvstate["n"] += len(units)

        def s_gates(st):
            wi, ncols = st["wi"], st["ncols"]
            OBT, rOBT = obT[0]
            (wba, wbb), r_wbr = get_piece(wi, "br")
            for m in range(8):
                (wga, wgb), r_wg = get_piece(wi, "g%d" % (m // 4))
                mm = m % 4
                tgs = []
                for gi, wg_ in enumerate((wga, wgb)):
                    b = nb()
                    pt, rpt = PB(b)
                    for f in range(8):
                        pe.op(lambda e: e.matmul(pt[:, 0:ncols], lhsT=wg_[:, f, mm * 128:(mm + 1) * 128], rhs=hT[:, f, 0:ncols],
                                                 start=(f == 0), stop=(f == 7)), reads=[r_wg] + r_hT, writes=[rpt],
                              inc=(f == 7))
                    tg_, rtg = tg[gi]
                    act.op(lambda e: e.activation(out=tg_[:, 0:ncols], in_=pt[:, 0:ncols], func=AF.Tanh,
                                                  bias=bg[:, gi * 8 + m:gi * 8 + m + 1], scale=0.5), reads=[rpt, r_bg],
                           writes=[rtg])
                    tgs.append((tg_, rtg))
                for gi, (wb_, src, rsrc) in enumerate(((wba, oaT, r_oaT), (wbb, OBT, rOBT))):
                    b = nb()
                    pt, rpt = PB(b)
                    for kc in range(4):
                        pe.op(lambda e: e.matmul(pt[:, 0:ncols], lhsT=wb_[:, kc, m * 128:(m + 1) * 128], rhs=src[:, kc, 0:ncols],
                                                 start=(kc == 0), stop=(kc == 3)), reads=[r_wbr, rsrc], writes=[rpt],
                              inc=(kc == 3))
                    tg_, rtg = tgs[gi]
                    tm_, rtm = tm[gi]
                    dve.op(lambda e: e.scalar_tensor_tensor(out=tm_[:, 0:ncols], in0=tg_[:, 0:ncols], scalar=1.0,
                                                            in1=pt[:, 0:ncols], op0=ALU.add, op1=ALU.mult),
                           reads=[rtg, rpt], writes=[rtm])
                dve.op(lambda e: e.tensor_tensor(out=mT[:, m, 0:ncols], in0=tm[0][0][:, 0:ncols], in1=tm[1][0][:, 0:ncols],
                                                 op=ALU.add), reads=[tm[0][1], tm[1][1]], writes=[r_mT])
            release_piece(wi, "g1")

        def s_wo_norm2(st):
            wi, ncols, subs = st["wi"], st["ncols"], st["subs"]
            X, rX = st["X"]
            (wo,), r_wo = get_piece(wi, "o")
            s_, rs_ = sst["n2"]
            o_, ro_ = rst["n2"]
            for j, sz in enumerate(subs):
                for half in range(2):
                    b = nb()
                    pt, rpt = PB(b)
                    for kc in range(8):
                        pe.op(lambda e: e.matmul(pt[0:sz, :], lhsT=mT[:, kc, j * 128:j * 128 + sz],
                                                 rhs=wo[:, kc, half * 512:(half + 1) * 512], start=(kc == 0), stop=(kc == 7)),
                              reads=[r_wo, r_mT], writes=[rpt], inc=(kc == 7))
                    dve.op(lambda e: e.tensor_tensor(out=X[0:sz, j, half * 512:(half + 1) * 512], in0=pt[0:sz, :],
                                                     in1=X[0:sz, j, half * 512:(half + 1) * 512], op=ALU.add),
                           reads=[rpt, rX[j]], writes=[rX[j]])
                jt, rj = jk()
                act.op(lambda e: e.activation(out=jt[0:sz, :], in_=X[0:sz, j, :], func=AF.Square,
                                              accum_out=s_[0:sz, j:j + 1]), reads=[rX[j]], writes=[rj, rs_[j]])
                rstd2("n2", j, sz, D)
                kb.scale_cast(xn[0:sz, j, :], X[0:sz, j, :], o_[0:sz, j:j + 1], [rX[j], ro_[j]], [r_xn[j]])
            release_piece(wi, "o")
            norm_b(st)
            if st.get("maskL"):
                dve.op(lambda e: e.tensor_scalar(out=hT[:, :, 0:1], in0=hT[:, :, 0:1], scalar1=hm[:, 0:1], scalar2=None,
                                                 op0=ALU.mult), reads=[r_hT[0], r_hm], writes=[r_hT[0]])
            if st.get("maskR"):
                dve.op(lambda e: e.tensor_scalar(out=hT[:, :, ncols - 1:ncols], in0=hT[:, :, ncols - 1:ncols],
                                                 scalar1=hm[:, 1:2], scalar2=None, op0=ALU.mult),
                       reads=[r_hT[len(subs) - 1], r_hm], writes=[r_hT[len(subs) - 1]])

        def s_up(st):
            wi, ncols = st["wi"], st["ncols"]
            for c in range(NFF):
                (wu, wg_), r_wup = get_piece(wi, "up%d" % (c // 4))
                cc = c % 4
                bu = nb()
                pu, rpu = PB(bu)
                for f in range(8):
                    pe.op(lambda e: e.matmul(pu[:, 0:ncols], lhsT=wu[:, f, cc * 128:(cc + 1) * 128], rhs=hT[:, f, 0:ncols],
                                             start=(f == 0), stop=(f == 7)), reads=[r_wup] + r_hT, writes=[rpu], inc=(f == 7))
                bgk = nb()
                pg, rpg = PB(bgk)
                for f in range(8):
                    pe.op(lambda e: e.matmul(pg[:, 0:ncols], lhsT=wg_[:, f, cc * 128:(cc + 1) * 128], rhs=hT[:, f, 0:ncols],
                                             start=(f == 0), stop=(f == 7)), reads=[r_wup] + r_hT, writes=[rpg], inc=(f == 7))
                tc_, rtc = tcv[c % 2]
                th_, rth = th[c % 2]
                act.op(lambda e: e.activation(out=tc_[:, 0:ncols], in_=pg[:, 0:ncols], func=AF.Identity,
                                              bias=cwh[:, 3 * NFF + c:3 * NFF + c + 1], scale=cwh[:, NFF + c:NFF + c + 1]),
                       reads=[rpg, r_cwh], writes=[rtc])
                dve.op(lambda e: e.scalar_tensor_tensor(out=tc_[:, 1:ncols], in0=pg[:, 0:ncols - 1], scalar=cwh[:, c:c + 1],
                                                        in1=tc_[:, 1:ncols], op0=ALU.mult, op1=ALU.add),
                       reads=[rpg, rtc, r_cwh], writes=[rtc])
                dve.op(lambda e: e.scalar_tensor_tensor(out=tc_[:, 0:ncols - 1], in0=pg[:, 1:ncols],
                                                        scalar=cwh[:, 2 * NFF + c:2 * NFF + c + 1], in1=tc_[:, 0:ncols - 1],
                                                        op0=ALU.mult, op1=ALU.add), reads=[rpg, rtc, r_cwh], writes=[rtc])
                act.op(lambda e: e.activation(out=th_[:, 0:ncols], in_=tc_[:, 0:ncols], func=AF.Silu), reads=[rtc],
                       writes=[rth])
                dve.op(lambda e: e.tensor_tensor(out=aT[:, c, 0:ncols], in0=pu[:, 0:ncols], in1=th_[:, 0:ncols], op=ALU.mult),
                       reads=[rpu, rth], writes=[r_aT])
                if c % 4 == 3 or c == NFF - 1:
                    release_piece(wi, "up%d" % (c // 4))

        def s_down_final(st):
            wi, ncols, subs, L, n_out = st["wi"], st["ncols"], st["subs"], st["L"], st["n_out"]
            X, rX = st["X"]
            dn = [get_piece(wi, "dn%d" % g) for g in range(3)]
            s_, rs_ = sst["f"]
            o_, ro_ = rst["f"]
            for j, sz in enumerate(subs):
                for half in range(2):
                    b = nb()
                    pt, rpt = PB(b)
                    for c in range(NFF):
                        (wd,), r_wd = dn[c // 8]
                        pe.op(lambda e: e.matmul(pt[0:sz, :], lhsT=aT[:, c, j * 128:j * 128 + sz],
                                                 rhs=wd[:, c % 8, half * 512:(half + 1) * 512], start=(c == 0),
                                                 stop=(c == NFF - 1)), reads=[r_wd, r_aT], writes=[rpt], inc=(c == NFF - 1))
                    dve.op(lambda e: e.tensor_tensor(out=X[0:sz, j, half * 512:(half + 1) * 512], in0=pt[0:sz, :],
                                                     in1=X[0:sz, j, half * 512:(half + 1) * 512], op=ALU.add),
                           reads=[rpt, rX[j]], writes=[rX[j]])
                jt, rj = jk()
                act.op(lambda e: e.activation(out=jt[0:sz, :], in_=X[0:sz, j, :], func=AF.Square,
                                              accum_out=s_[0:sz, j:j + 1]), reads=[rX[j]], writes=[rj, rs_[j]])
                rstd2("f", j, sz, D)
                dve.op(lambda e: e.scalar_tensor_tensor(out=X[0:sz, j, :], in0=X[0:sz, j, :], scalar=o_[0:sz, j:j + 1],
                                                        in1=nfb[0:sz, :], op0=ALU.mult, op1=ALU.mult),
                       reads=[rX[j], ro_[j], r_nfb], writes=[rX[j]])
                p0, p1 = max(L - 128 * j, 0), min(L + n_out - 128 * j, sz)
                if p1 > p0:
                    o0 = 128 * j + p0 - L
                    pool.dma(st["out"][o0:o0 + (p1 - p0), :], X[p0:p1, j, :], rX[j], None, load=False, semres=Xst[wi % NX][j])
            release_piece(wi, "dn2")

        wins = []
        for si, nm in enumerate(("p0", "p1")):
            for (o_s, o_e) in split_windows(SP_LEN):
                c_s, c_e = max(o_s - 1, 0), min(o_e + 1, SP_LEN)
                wins.append(dict(ncols=c_e - c_s, L=o_s - c_s, n_out=o_e - o_s, x=xp[si, c_s:c_e, :],
                                 ropeq=ropeq_p[:, c_s:c_e], ob=seqs[nm].OB[:, c_s:c_e], r_ob=seqs[nm].r_ob, kv=seqs[nm],
                                 S_kv=SP_LEN, out=yp[si, o_s:o_e, :]))
        if not (debug and debug.get("skip_sample")):
            sw = split_windows(OWN)
            for k_, (o_s, o_e) in enumerate(sw):
                c_s, c_e = HALO_LO + o_s - 1, HALO_LO + o_e + 1
                wins.append(dict(ncols=c_e - c_s, L=1, n_out=o_e - o_s, x=xo[c_s:c_e, :], ropeq=ropeq_s[:, c_s:c_e],
                                 ob=seqs["so"].OB[:, c_s:c_e], r_ob=seqs["so"].r_ob, kv=seqs["sm"], S_kv=SS_LEN,
                                 out=yo[o_s:o_e, :], maskL=(k_ == 0), maskR=(k_ == len(sw) - 1)))
        if debug and debug.get("win_list") is not None:
            wins = [wins[i] for i in debug["win_list"]]
        pstate["total"] = len(wins) * NPIECE
        sts = [w_state(wi, W) for wi, W in enumerate(wins)]
        s_load(sts[0])
        norm_a(sts[0], "h")
        norm_b(sts[0])
        for wi, st in enumerate(sts):
            nxt = sts[wi + 1] if wi + 1 < len(sts) else None
            s_cq(st)
            s_q(st)
            if nxt is not None:
                s_load(nxt)
            s_attn(st)
            s_gates(st)
            s_wo_norm2(st)
            s_up(st)
            if nxt is not None:
                norm_a(nxt, "h")
            s_down_final(st)
            if nxt is not None:
                norm_b(nxt)
        kb.barrier()

    kb.finish()
    return nc


def _rope_tables():
    inv = (np.float32(10000.0) ** (-np.arange(0, 32, 2, dtype=np.float32) / np.float32(32))).astype(np.float32)
    ang = (np.arange(SS_LEN, dtype=np.float32)[:, None] * inv[None, :]).astype(np.float32)
    c, s = np.cos(ang).astype(np.float32), np.sin(ang).astype(np.float32)
    return c, s


def make_inputs(inputs, debug=None):
    f = lambda a: np.ascontiguousarray(np.asarray(a, dtype=np.float32))
    x_prompt, x_sample = f(inputs["x_prompt"]), f(inputs["x_sample"])[0]
    c, s = _rope_tables()
    ropek = np.concatenate([c, c, -s, s], axis=1).astype(np.float32)
    ropeq_full = np.ascontiguousarray(ropek.T)
    rpb = f(inputs["rpb"])[0]
    ptab, pidx = na_tables(rpb, "p", 0)
    xs_pad = np.zeros((SS_LEN + 2 * 512, D), np.float32)
    xs_pad[512:512 + SS_LEN] = x_sample
    rq_pad = np.zeros((64, SS_LEN + 2 * 512), np.float32)
    rq_pad[:, 512:512 + SS_LEN] = ropeq_full
    shared = dict(
        xs=x_sample,
        w_in=f(inputs["w_in"])[0], w_qb=f(inputs["w_qb"])[0], w_kvb=f(inputs["w_kvb"])[0],
        w_br_a=f(inputs["w_br_a"])[0], w_br_b=f(inputs["w_br_b"])[0], w_o=f(inputs["w_o"])[0],
        w_up=f(inputs["w_up"])[0], w_down=f(inputs["w_down"])[0],
        norm_mix=f(inputs["norm_mix"])[0], norm_q=f(inputs["norm_q"])[0], norm_kv=f(inputs["norm_kv"])[0],
        norm_ffn=f(inputs["norm_ffn"])[0], norm_final=f(inputs["norm_final"]),
        b_gate=f(inputs["b_gate"])[0], conv_w=f(inputs["conv_w"])[0], conv_b=f(inputs["conv_b"])[0],
        ropek=ropek, ropeq_p=np.ascontiguousarray(ropeq_full[:, :SP_LEN]), ptab=ptab,
        ident=np.eye(128, dtype=np.float32).astype(ml_dtypes.bfloat16),
    )
    in_maps, sidx = [], None
    n_stab = None
    stabs = []
    for c_ in range(NCORES):
        st, si = na_tables(rpb, "s", c_)
        stabs.append(st)
        sidx = si if sidx is None else sidx
    n_stab = max(t.shape[0] for t in stabs)
    for c_ in range(NCORES):
        o0 = c_ * OWN
        st = stabs[c_]
        if st.shape[0] < n_stab:
            st = np.concatenate([st, np.zeros((n_stab - st.shape[0],) + st.shape[1:], st.dtype)], 0)
        hmask = np.ones((128, 2), np.float32)
        if c_ == 0:
            hmask[:, 0] = 0.0
        if c_ == NCORES - 1:
            hmask[:, 1] = 0.0
        m = dict(shared)
        m.update(
            xp=np.ascontiguousarray(x_prompt[2 * c_:2 * c_ + 2]),
            xo=np.ascontiguousarray(xs_pad[512 + o0 - HALO_LO:512 + o0 - HALO_LO + XO_LEN]),
            ropeq_s=np.ascontiguousarray(rq_pad[:, 512 + o0 - HALO_LO:512 + o0 - HALO_LO + XO_LEN]),
            stab=st, hmask=hmask,
        )
        in_maps.append(m)
    return in_maps, (ptab.shape[0], pidx, n_stab, sidx)


def kernel(**inputs):
    in_maps, (n_ptab, pidx, n_stab, sidx) = make_inputs(inputs)
    nc = build_program(n_ptab, pidx, n_stab, sidx)
    res = run_bass_kernel_spmd(nc, in_maps, core_ids=list(range(NCORES)))
    yp = np.concatenate([r["yp"] for r in res.results], axis=0)
    yo = np.concatenate([r["yo"] for r in res.results], axis=0)[None]
    return (np.ascontiguousarray(yp, dtype=np.float32), np.ascontiguousarray(yo, dtype=np.float32))
```

```python
import math
from contextlib import ExitStack

import numpy as np
import ml_dtypes

import concourse.bass as bass
import concourse.mybir as mybir
from concourse.bass_utils import run_bass_kernel_spmd

F32 = mybir.dt.float32
BF16 = mybir.dt.bfloat16
AF = mybir.ActivationFunctionType
ALU = mybir.AluOpType

NCORES = 8
D = 1024
SP_LEN = 4096
SS_LEN = 16384
OWN = SS_LEN // NCORES
HALO_LO = 512
XO_LEN = 3072
FOFF = HALO_LO // 64
GRID_W = 64
H = 8
IN_COLS = 4640
DFF = 2816
NFF = DFF // 128
EPS = 1e-6
MASKV = -30000.0
KC = 1024
NBLK = KC // 128


class Ev:
    __slots__ = ("sem", "val", "clock")

    def __init__(self, sem, val, clock):
        self.sem, self.val, self.clock = sem, val, clock


class Res:
    def __init__(self, name, dsem=None, multi=False):
        self.name = name
        self.w = {}
        self.r = {}
        self.dsem = dsem
        self.dcount = 0
        self.multi = multi
        self.alias = []
        self.scratch = False


class Eng:
    def __init__(self, kb, name, eng, sem, is_pe=False):
        self.kb, self.name, self.eng, self.sem = kb, name, eng, sem
        self.count = 0
        self.known = {}
        self.is_pe = is_pe
        self.pend_r, self.pend_w = [], []

    def _need(self, ev, need):
        if ev is None or self.known.get(ev.sem, 0) >= ev.val:
            return
        cur = need.get(ev.sem)
        if cur is None or cur.val < ev.val:
            need[ev.sem] = ev

    def _sync(self, reads, writes):
        need = {}
        for r in reads:
            for ev in r.w.values():
                self._need(ev, need)
        for w in writes:
            if w.scratch:
                continue
            if not w.multi:
                for ev in w.w.values():
                    self._need(ev, need)
            for ev in w.r.values():
                self._need(ev, need)
            for al in w.alias:
                for ev in al.w.values():
                    self._need(ev, need)
                for ev in al.r.values():
                    self._need(ev, need)
        for sem, ev in need.items():
            if self.is_pe and sem is self.sem:
                continue
            self.eng.wait_ge(sem, ev.val)
            if self.known.get(sem, 0) < ev.val:
                self.known[sem] = ev.val
            for s, v in ev.clock.items():
                if self.known.get(s, 0) < v:
                    self.known[s] = v

    def _flush(self, ev):
        for r in self.pend_r:
            r.r[ev.sem] = ev
        for w in self.pend_w:
            if w.multi:
                w.w[ev.sem] = ev
            else:
                w.w = {ev.sem: ev}
                w.r = {}
        self.pend_r, self.pend_w = [], []

    def op(self, fn, reads=(), writes=(), inc=True):
        self._sync(reads, writes)
        ins = fn(self.eng)
        self.pend_r += list(reads)
        self.pend_w += list(writes)
        if inc:
            self.count += 1
            ins.then_inc(self.sem, 1)
            self._flush(Ev(self.sem, self.count, dict(self.known)))
        return ins

    def dma(self, out, in_, sres, dres=None, load=True, semres=None, **kw):
        if load:
            reads, writes = ([dres] if dres is not None else []), [sres]
        else:
            reads, writes = [sres], ([dres] if dres is not None else [])
        self._sync(reads, writes)
        semres = semres or sres
        semres.dcount += 16
        self.eng.dma_start(out=out, in_=in_, **kw).then_inc(semres.dsem, 16)
        ev = Ev(semres.dsem, semres.dcount, dict(self.known))
        for r in reads:
            r.r[ev.sem] = ev
        for w in writes:
            if w.multi:
                w.w[ev.sem] = ev
            else:
                w.w = {ev.sem: ev}
                w.r = {}


class KB:
    def __init__(self, nc):
        self.nc = nc
        self.stack = ExitStack()
        self.all_res = []
        mk = lambda n, e, pe=False: Eng(self, n, e, self.stack.enter_context(nc.semaphore("s_" + n)), pe)
        self.pe = mk("pe", nc.tensor, True)
        self.act = mk("act", nc.scalar)
        self.dve = mk("dve", nc.vector)
        self.pool = mk("pool", nc.gpsimd)
        self.sp = mk("sp", nc.sync)
        self.engs = [self.pe, self.act, self.dve, self.pool, self.sp]
        self._rr = 0

    def sb(self, name, shape, dt, dma=False):
        t = self.stack.enter_context(self.nc.sbuf_tensor("sb_" + name, list(shape), dt))
        return t, self.res(name, dma)

    def ps(self, name):
        t = self.stack.enter_context(self.nc.psum_tensor(name, [128, 512], F32))
        return t, self.res(name)

    def res(self, name, dma=False, multi=False):
        ds = self.stack.enter_context(self.nc.semaphore("d_" + name)) if dma else None
        r = Res(name, ds, multi)
        self.all_res.append(r)
        return r

    def scale_cast(self, out_ap, in_ap, scal_ap, reads, writes):
        self._rr += 1
        if self._rr % 2:
            self.dve.op(lambda e: e.tensor_scalar(out=out_ap, in0=in_ap, scalar1=scal_ap, scalar2=None, op0=ALU.mult),
                        reads=reads, writes=writes)
        else:
            self.act.op(lambda e: e.activation(out=out_ap, in_=in_ap, func=AF.Copy, scale=scal_ap), reads=reads,
                        writes=writes)

    def barrier(self):
        for e in self.engs:
            for o in self.engs:
                if o is not e and o.count > e.known.get(o.sem, 0):
                    e.eng.wait_ge(o.sem, o.count)
                    e.known[o.sem] = o.count
            for r in self.all_res:
                if r.dsem is not None and r.dcount > e.known.get(r.dsem, 0):
                    e.eng.wait_ge(r.dsem, r.dcount)
                    e.known[r.dsem] = r.dcount

    def finish(self):
        e = self.sp
        for o in self.engs:
            if o is not e and o.count > e.known.get(o.sem, 0):
                e.eng.wait_ge(o.sem, o.count)
        for r in self.all_res:
            if r.dsem is not None and r.dcount > e.known.get(r.dsem, 0):
                e.eng.wait_ge(r.dsem, r.dcount)


def split_windows(n_tok, max_out=510):
    n = math.ceil(n_tok / max_out)
    base, rem = divmod(n_tok, n)
    out, s = [], 0
    for i in range(n):
        sz = base + (1 if i < rem else 0)
        out.append((s, s + sz))
        s += sz
    return out


def na_row_blocks(kind, r):
    if kind == "p":
        rows = SP_LEN // GRID_W
        rs = min(max(r - 4, 0), rows - 8)
        a0 = rs - (rs % 2)
        return list(range(a0, rs + 8, 2))
    lo, hi = r - 4, r + 3
    if 0 <= r < 4:
        hi = max(hi, 7)
    if 28 < r <= 31:
        lo = min(lo, 24)
    lo_f, hi_f = lo + FOFF, hi + FOFF
    a0 = lo_f - (lo_f % 2)
    return list(range(a0, hi_f + 1, 2))


def na_groups(kind):
    if kind == "p":
        return [list(range(g, g + 4)) for g in range(0, SP_LEN // GRID_W, 4)]
    return [list(range(g, g + 4)) for g in range(-4, OWN // GRID_W + 4, 4)]


def na_group_blocks(kind, rows4):
    b = set()
    for r in rows4:
        b |= set(na_row_blocks(kind, r))
    return sorted(b)


def na_tables(rpb, kind, core):
    rows = SP_LEN // GRID_W if kind == "p" else SS_LEN // GRID_W
    nloc = rows if kind == "p" else OWN // GRID_W
    qc = np.arange(GRID_W)
    cs = np.clip(qc - 8, 0, GRID_W - 16)

    def row_tab(r, a):
        rg = r if kind == "p" else core * nloc + r
        rs = min(max(rg - 4, 0), rows - 8)
        if not (0 <= rg < rows):
            rs = rg - 4
        ag = a if kind == "p" else a - FOFF + core * nloc
        if a not in na_row_blocks(kind, r):
            return "masked", None
        key = tuple((ag + i - rg, rs <= ag + i < rs + 8) for i in range(2))
        if kind == "s" and (0 <= r < 4 or 28 < r <= 31):
            key = ("edge", r, a)

        def build():
            t = np.full((128, H, GRID_W), MASKV, np.float32)
            for i in range(2):
                kr = ag + i
                if not (rs <= kr < rs + 8):
                    continue
                dr = kr - rg + 7
                for kcol in range(GRID_W):
                    ok = (kcol >= cs) & (kcol < cs + 16)
                    dc = kcol - qc + 15
                    t[i * 64 + kcol][:, ok] = rpb[:, dr, dc[ok]]
            return t
        return key, build

    tabs, index, cache, rcache = [], {}, {}, {}
    for rows4 in na_groups(kind):
        for a in na_group_blocks(kind, rows4):
            parts = [row_tab(r, a) for r in rows4]
            key = tuple((r - rows4[0], k_) for r, (k_, _) in zip(rows4, parts))
            if key not in cache:
                t4 = np.full((128, H, 4 * GRID_W), MASKV, np.float32)
                for i, (k_, build) in enumerate(parts):
                    if build is not None:
                        if k_ not in rcache:
                            rcache[k_] = build()
                        t4[:, :, i * 64:(i + 1) * 64] = rcache[k_]
                cache[key] = len(tabs)
                tabs.append(t4)
            index[(rows4[0], a)] = cache[key]
    return np.stack(tabs).astype(ml_dtypes.bfloat16), index


def na_table_index(kind):
    raise NotImplementedError


def build_program(n_ptab, ptab_index, n_stab, stab_index, debug=None):
    nc = bass.Bass("TRN2", target_bir_lowering=False)
    kb = KB(nc)
    pe, act, dve, pool, sp = kb.pe, kb.act, kb.dve, kb.pool, kb.sp

    def din(name, shape, dt=F32):
        return nc.dram_tensor(name, list(shape), dt, kind="ExternalInput").ap()

    def dout(name, shape, dt=F32):
        return nc.dram_tensor(name, list(shape), dt, kind="ExternalOutput").ap()

    def dscr(name, shape, dt=BF16):
        kind = "ExternalOutput" if (debug and name in debug) else "Internal"
        return nc.dram_tensor(name, list(shape), dt, kind=kind).ap()

    xp = din("xp", [2, SP_LEN, D])
    xs = din("xs", [SS_LEN, D])
    xo = din("xo", [XO_LEN, D])
    w_in = din("w_in", [D, IN_COLS])
    w_qb = din("w_qb", [768, 768])
    w_kvb = din("w_kvb", [256, 1024])
    w_br_a = din("w_br_a", [512, D])
    w_br_b = din("w_br_b", [512, D])
    w_o = din("w_o", [D, D])
    w_up = din("w_up", [D, 2 * DFF])
    w_down = din("w_down", [DFF, D])
    norm_mix = din("norm_mix", [D])
    norm_q = din("norm_q", [768])
    norm_kv = din("norm_kv", [256])
    norm_ffn = din("norm_ffn", [D])
    norm_final = din("norm_final", [D])
    b_gate = din("b_gate", [2, D])
    conv_w = din("conv_w", [3, DFF])
    conv_b = din("conv_b", [DFF])
    ropek = din("ropek", [SS_LEN, 64])
    ropeq_p = din("ropeq_p", [64, SP_LEN])
    ropeq_s = din("ropeq_s", [64, XO_LEN])
    ptab = din("ptab", [n_ptab, 128, H, 256], BF16)
    stab = din("stab", [n_stab, 128, H, 256], BF16)
    hmask = din("hmask", [128, 2])
    ident_d = din("ident", [128, 128], BF16)

    yp = dout("yp", [2, SP_LEN, D])
    yo = dout("yo", [OWN, D])

    WIN = dscr("WIN", [D, IN_COLS])
    WQB = dscr("WQB", [768, 1024])
    WKVB = dscr("WKVB", [256, 1024])
    WBA = dscr("WBA", [512, D])
    WBB = dscr("WBB", [512, D])
    WO = dscr("WO", [D, D])
    WUP = dscr("WUP", [D, 2 * DFF])
    WDN = dscr("WDN", [DFF, D])
    r_w = kb.res("wscr", multi=True)

    class Seq:
        pass

    seqs = {}
    for nm, S_kv, T_na in (("p0", SP_LEN, SP_LEN), ("p1", SP_LEN, SP_LEN), ("sm", SS_LEN, 0), ("so", 0, XO_LEN)):
        s = Seq()
        s.name = nm
        if S_kv:
            s.KN = dscr("KN_" + nm, [512, S_kv])
            s.KP = dscr("KP_" + nm, [32, S_kv])
            s.VD = dscr("VD_" + nm, [H, 128, S_kv // 128, 128])
            s.r_kv = kb.res("kv_" + nm, multi=True)
        if T_na:
            s.NQ = dscr("NQ_" + nm, [512, T_na])
            s.NK = dscr("NK_" + nm, [512, T_na])
            s.NV = dscr("NV_" + nm, [T_na, 1024])
            s.OB = dscr("OB_" + nm, [512, T_na])
            s.r_na = kb.res("na_" + nm, multi=True)
            s.r_ob = kb.res("ob_" + nm, multi=True)
        seqs[nm] = s

    ident, r_ident = kb.sb("ident", [128, 128], BF16, dma=True)
    gvec, r_gvec = kb.sb("gvec", [128, 8 + 6 + 2 + 8], F32, dma=True)
    gsc, r_gsc = kb.sb("gsc", [128, 8 + 6], F32)
    bg, r_bg = kb.sb("bg", [128, 16], F32, dma=True)
    cw, r_cw = kb.sb("cw", [128, 4 * NFF], F32, dma=True)
    cwh, r_cwh = kb.sb("cwh", [128, 4 * NFF], F32)
    nfb, r_nfb = kb.sb("nfb", [128, D], F32, dma=True)
    epsT, r_eps = kb.sb("epsT", [128, 1], F32)
    hm, r_hm = kb.sb("hm", [128, 2], F32, dma=True)
    pdbl = [kb.stack.enter_context(nc.psum_tensor("pdbl%d" % i, [128, 1024], F32)) for i in range(4)]
    pbank = [(pdbl[i // 2][:, (i % 2) * 512:(i % 2 + 1) * 512], kb.res("pb%d" % i)) for i in range(8)]

    def PB(i):
        return pbank[i][0], pbank[i][1]

    def PBbf(i):
        return pbank[i][0].bitcast(BF16)

    sp.dma(ident[:], ident_d[:, :], r_ident)
    with nc.allow_non_contiguous_dma(reason="tiny one-time vector loads"):
        sp.dma(gvec[:, 0:8], norm_mix.rearrange("(c p) -> p c", p=128), r_gvec)
        sp.dma(gvec[:, 8:14], norm_q.rearrange("(c p) -> p c", p=128), r_gvec)
        sp.dma(gvec[:, 14:16], norm_kv.rearrange("(c p) -> p c", p=128), r_gvec)
        sp.dma(gvec[:, 16:24], norm_ffn.rearrange("(c p) -> p c", p=128), r_gvec)
        sp.dma(bg[:, :].rearrange("p (t c) -> p t c", t=2), b_gate.rearrange("t (c p) -> p t c", p=128), r_bg)
        sp.dma(cw[:, 0:3 * NFF].rearrange("p (t c) -> p t c", t=3), conv_w.rearrange("t (c p) -> p t c", p=128), r_cw)
        sp.dma(cw[:, 3 * NFF:4 * NFF], conv_b.rearrange("(c p) -> p c", p=128), r_cw)
    sp.dma(nfb[:], norm_final.partition_broadcast(128), r_nfb)
    sp.dma(hm[:], hmask[:, :], r_hm)
    dve.op(lambda e: e.memset(epsT[:], EPS), writes=[r_eps])
    dve.op(lambda e: e.tensor_scalar(out=gsc[:, 0:8], in0=gvec[:, 0:8], scalar1=0.125, scalar2=None, op0=ALU.mult),
           reads=[r_gvec], writes=[r_gsc])
    dve.op(lambda e: e.tensor_scalar(out=gsc[:, 8:14], in0=gvec[:, 8:14], scalar1=96.0 ** -0.5, scalar2=None,
                                     op0=ALU.mult), reads=[r_gvec], writes=[r_gsc])
    dve.op(lambda e: e.tensor_scalar(out=bg[:, :], in0=bg[:, :], scalar1=0.5, scalar2=None, op0=ALU.mult),
           reads=[r_bg], writes=[r_bg])
    dve.op(lambda e: e.tensor_copy(out=cwh[:, :], in_=cw[:, :]), reads=[r_cw], writes=[r_cwh])

    def make_staging(stk, tag, ns):
        si, so = [], []
        for i in range(ns):
            t = stk.enter_context(nc.sbuf_tensor("w0i%s%d" % (tag, i), [128, 2048], F32))
            si.append((t, kb.res("w0i%s%d" % (tag, i), dma=True)))
            t2 = stk.enter_context(nc.sbuf_tensor("w0o%s%d" % (tag, i), [128, 2048], BF16))
            so.append((t2, kb.res("w0o%s%d" % (tag, i), dma=True)))
        return si, so

    def prep(stg, cnt, src, dst, K, ncols, gain=None, const=None, blocks=None, special=None):
        stg_in, stg_out = stg
        ns = len(stg_in)
        blocks = blocks or [(c0, min(c0 + 2048, ncols)) for c0 in range(0, ncols, 2048)]
        for kc in range(K // 128):
            for (c0, c1, *extra) in blocks:
                i = cnt[0] % ns
                cnt[0] += 1
                ti, ri = stg_in[i]
                to, ro = stg_out[i]
                w = c1 - c0
                sp.dma(ti[:, 0:w], src[kc * 128:(kc + 1) * 128, c0:c1], ri)
                g = gain(kc, extra) if gain else None
                eng = [dve, act][cnt[0] % 2] if special is None else dve
                if special is not None:
                    special(eng, ti, to, g, ri, ro)
                    dw = special.out_cols
                    (pool if i % 2 else sp).dma(dst[kc * 128:(kc + 1) * 128, 0:dw], to[:, 0:dw], ro, r_w, load=False)
                    yield
                    continue
                if eng is act:
                    if g is not None:
                        eng.op(lambda e: e.activation(out=to[:, 0:w], in_=ti[:, 0:w], func=AF.Copy, scale=g),
                               reads=[ri, r_gvec, r_gsc], writes=[ro])
                    else:
                        eng.op(lambda e: e.activation(out=to[:, 0:w], in_=ti[:, 0:w], func=AF.Copy,
                                                      scale=float(const or 1.0)), reads=[ri], writes=[ro])
                else:
                    if g is not None:
                        eng.op(lambda e: e.tensor_scalar(out=to[:, 0:w], in0=ti[:, 0:w], scalar1=g, scalar2=None,
                                                         op0=ALU.mult), reads=[ri, r_gvec, r_gsc], writes=[ro])
                    else:
                        eng.op(lambda e: e.tensor_scalar(out=to[:, 0:w], in0=ti[:, 0:w],
                                                         scalar1=float(const or 1.0), scalar2=None, op0=ALU.mult),
                               reads=[ri], writes=[ro])
                q = pool if i % 2 else sp
                q.dma(dst[kc * 128:(kc + 1) * 128, c0:c1], to[:, 0:w], ro, r_w, load=False)
                yield

    def sp_qb(eng, ti, to, g, ri, ro):
        iv = ti[:, 0:768].rearrange("p (h c) -> p h c", c=96)
        ov = to[:, 0:1024].rearrange("p (h c) -> p h c", c=128)
        for (o0, o1, i0_, i1_) in ((0, 96, 0, 96), (96, 112, 80, 96), (112, 128, 64, 80)):
            eng.op(lambda e: e.tensor_scalar(out=ov[:, :, o0:o1], in0=iv[:, :, i0_:i1_], scalar1=g, scalar2=None,
                                             op0=ALU.mult), reads=[ri, r_gsc], writes=[ro])
    sp_qb.out_cols = 1024

    def sp_kvb(eng, ti, to, g, ri, ro):
        iv = ti[:, 0:1024].rearrange("p (h t d) -> p h t d", t=2, d=64)
        ov = to[:, 0:1024].rearrange("p (t h d) -> p t h d", t=2, d=64)
        for t in range(2):
            eng.op(lambda e: e.tensor_scalar(out=ov[:, t, :, :], in0=iv[:, :, t, :], scalar1=g, scalar2=None,
                                             op0=ALU.mult), reads=[ri, r_gvec], writes=[ro])
    sp_kvb.out_cols = 1024

    win_gain = lambda kc, ex: (gsc[:, kc:kc + 1] if ex and ex[0] else gvec[:, kc:kc + 1])

    def prep_rest(stg, cnt):
        yield from prep(stg, cnt, w_in, WIN, D, IN_COLS, gain=win_gain, blocks=[(0, 768), (2592, 4640)])
        yield from prep(stg, cnt, w_qb, WQB, 768, 768, gain=lambda kc, ex: gsc[:, 8 + kc:9 + kc], blocks=[(0, 768)],
                        special=sp_qb)
        yield from prep(stg, cnt, w_br_a, WBA, 512, D, const=1.0)
        yield from prep(stg, cnt, w_br_b, WBB, 512, D, const=1.0)
        yield from prep(stg, cnt, w_o, WO, D, D, const=0.5)
        yield from prep(stg, cnt, w_up, WUP, D, 2 * DFF, gain=lambda kc, ex: gvec[:, 16 + kc:17 + kc])
        yield from prep(stg, cnt, w_down, WDN, DFF, D, const=1.0)

    with ExitStack() as st0:
        stg0 = make_staging(st0, "a", 4)
        cnt0 = [0]
        for _ in prep(stg0, cnt0, w_in, WIN, D, IN_COLS, gain=win_gain, blocks=[(768, 1056), (1056, 1568, True), (1568, 2592)]):
            pass
        for _ in prep(stg0, cnt0, w_kvb, WKVB, 256, 1024, gain=lambda kc, ex: gvec[:, 14 + kc:15 + kc], blocks=[(0, 1024)],
                      special=sp_kvb):
            pass
        if debug and debug.get("stop") in (0, 1):
            for _ in prep_rest(stg0, cnt0):
                pass
        kb.barrier()

    if debug and debug.get("stop") == 0:
        kb.finish()
        return nc

    def rms_rstd(ss_ap, out_ap, n, r_ss, r_out, width):
        r_ss = r_ss if isinstance(r_ss, list) else [r_ss]
        r_out = r_out if isinstance(r_out, list) else [r_out]
        act.op(lambda e: e.activation(out=out_ap, in_=ss_ap, func=AF.Sqrt, bias=epsT[:, 0:1], scale=1.0 / n),
               reads=r_ss + [r_eps], writes=r_out)
        dve.op(lambda e: e.reciprocal(out=out_ap, in_=out_ap), reads=r_out, writes=r_out)

    with ExitStack() as st1:
        def sb1(name, shape, dt, dma=False):
            t = st1.enter_context(nc.sbuf_tensor(name, list(shape), dt))
            return t, kb.res(name, dma)

        wkv, r_wkv = sb1("p1_wkv", [128, 8, 288], BF16, dma=True)
        wna, r_wna = sb1("p1_wna", [128, 8, 1536], BF16, dma=True)
        wkvb, r_wkvb = sb1("p1_wkvb", [128, 2, 1024], BF16, dma=True)
        sp.dma(wkv[:], WIN[:, 768:1056].rearrange("(c p) n -> p c n", p=128), r_wkv, r_w)
        sp.dma(wna[:], WIN[:, 1056:2592].rearrange("(c p) n -> p c n", p=128), r_wna, r_w)
        sp.dma(wkvb[:], WKVB[:, :].rearrange("(c p) n -> p c n", p=128), r_wkvb, r_w)
        xt = [sb1("p1_x%d" % i, [128, 4, D], F32, dma=True) for i in range(2)]
        rt = [sb1("p1_rope%d" % i, [128, 4, 64], F32, dma=True) for i in range(2)]
        dbl = {}
        for nm_, shp_, dt_ in (("xn", [128, 4, D], BF16), ("junk", [128, D], BF16), ("ss", [128, 8], F32),
                               ("rstd", [128, 8], F32), ("hT", [128, 8, 512], BF16), ("ckv", [128, 4, 288], F32),
                               ("ckvn", [128, 4, 256], BF16), ("ckvT", [128, 2, 512], BF16), ("t1", [128, 4, 32], F32),
                               ("t2", [128, 4, 32], F32), ("kpe", [128, 4, 32], BF16)):
            dbl[nm_] = [sb1("p1_%s%d" % (nm_, i), shp_, dt_) for i in range(2)]
        fine = {k_: [[kb.res("p1f_%s%d_%d" % (k_, i, c_)) for c_ in range(n_)] for i in range(2)]
                for k_, n_ in (("xn", 4), ("hT", 8), ("ckv", 4), ("ckvn", 4), ("ckvT", 2), ("ss", 8), ("rstd", 8))}
        for i in range(2):
            dbl["junk"][i][1].scratch = True
        junk2, r_junk2 = sb1("p1_junk2", [128, D], BF16)
        r_junk2.scratch = True
        NST = 2
        kpT = [sb1("p1_kpT%d" % i, [32, 512], BF16, dma=True) for i in range(NST)]
        knT = [sb1("p1_knT%d" % i, [128, 4, 512], BF16, dma=True) for i in range(NST)]
        vst = [sb1("p1_vst%d" % i, [128, H, 4, 128], BF16, dma=True) for i in range(NST)]
        nqT = [sb1("p1_nqT%d" % i, [128, 4, 512], BF16, dma=True) for i in range(NST)]
        nkT = [sb1("p1_nkT%d" % i, [128, 4, 512], BF16, dma=True) for i in range(NST)]
        nvs = [sb1("p1_nvs%d" % i, [128, 4, H, 128], BF16, dma=True) for i in range(NST)]
        for i in range(NST):
            dve.op(lambda e: e.memset(vst[i][0][:, :, :, 64:128], 1.0), writes=[vst[i][1]])
            dve.op(lambda e: e.memset(nvs[i][0][:, :, :, 64:128], 1.0), writes=[nvs[i][1]])
        bank_rr = [0]

        def nb():
            bank_rr[0] = (bank_rr[0] + 1) % 8
            return bank_rr[0]

        evac_rr = [0]

        def evac(out_ap, in_ap, r_out, r_in):
            evac_rr[0] += 1
            if evac_rr[0] % 3 == 0:
                act.op(lambda e: e.activation(out=out_ap, in_=in_ap, func=AF.Copy), reads=[r_in], writes=[r_out])
            else:
                dve.op(lambda e: e.tensor_copy(out=out_ap, in_=in_ap), reads=[r_in], writes=[r_out])

        def phase1_tile(x_rows, rope_rows, seq, col0, do_mla, do_na, ti):
            (xn, r_xn), (junk, r_junk), (ss, r_ss), (rstd, r_rstd), (hT, r_hT), (ckv, r_ckv), (ckvn, r_ckvn), \
                (ckvT, r_ckvT), (t1, r_t1), (t2, r_t2), (kpe, r_kpe) = [dbl[k_][ti % 2] for k_ in (
                    "xn", "junk", "ss", "rstd", "hT", "ckv", "ckvn", "ckvT", "t1", "t2", "kpe")]
            r_xn, r_hT, r_ckv, r_ckvn, r_ckvT, r_ss, r_rstd = [fine[k_][ti % 2] for k_ in (
                "xn", "hT", "ckv", "ckvn", "ckvT", "ss", "rstd")]
            X, rX = xt[ti % 2]
            sp.dma(X[:], x_rows.rearrange("(j p) d -> p j d", p=128), rX)
            if do_mla:
                RT, rRT = rt[ti % 2]
                sp.dma(RT[:], rope_rows.rearrange("(j p) d -> p j d", p=128), rRT)
            for j in range(4):
                if j % 2:
                    act.op(lambda e: e.activation(out=junk[:], in_=X[:, j, :], func=AF.Square, accum_out=ss[:, j:j + 1]),
                           reads=[rX], writes=[r_junk, r_ss[j]])
                else:
                    dve.op(lambda e: e.scalar_tensor_tensor(out=junk2[:], in0=X[:, j, :], scalar=1.0, in1=X[:, j, :],
                                                            op0=ALU.mult, op1=ALU.mult, accum_out=ss[:, j:j + 1]),
                           reads=[rX], writes=[r_junk2, r_ss[j]])
            rms_rstd(ss[:, 0:4], rstd[:, 0:4], D, r_ss[0:4], r_rstd[0:4], 4)
            for j in range(4):
                kb.scale_cast(xn[:, j, :], X[:, j, :], rstd[:, j:j + 1], [rX, r_rstd[j]], [r_xn[j]])
            yield
            for f in range(8):
                b = nb()
                pt, rpt = PB(b)
                ptb = PBbf(b)
                for j in range(4):
                    pe.op(lambda e: e.transpose(out=ptb[:, j * 128:(j + 1) * 128], in_=xn[:, j, f * 128:(f + 1) * 128],
                                                identity=ident[:]), reads=[r_xn[j], r_ident], writes=[rpt], inc=(j == 3))
                evac(hT[:, f, :], ptb[:, 0:512], r_hT[f], rpt)
            yield
            s = ti % NST
            if do_mla:
                for j in range(4):
                    b = nb()
                    pt, rpt = PB(b)
                    for f in range(8):
                        pe.op(lambda e: e.matmul(pt[:, 0:288], lhsT=hT[:, f, j * 128:(j + 1) * 128], rhs=wkv[:, f, :],
                                                 start=(f == 0), stop=(f == 7)), reads=[r_hT[f], r_wkv], writes=[rpt],
                              inc=(f == 7))
                    dve.op(lambda e: e.tensor_copy(out=ckv[:, j, :], in_=pt[:, 0:288]), reads=[rpt], writes=[r_ckv[j]])
                    dve.op(lambda e: e.scalar_tensor_tensor(out=junk[:, 0:256], in0=ckv[:, j, 0:256], scalar=1.0,
                                                            in1=ckv[:, j, 0:256], op0=ALU.mult, op1=ALU.mult,
                                                            accum_out=ss[:, 4 + j:5 + j]),
                           reads=[r_ckv[j]], writes=[r_junk, r_ss[4 + j]])
                rms_rstd(ss[:, 4:8], rstd[:, 4:8], 256, r_ss[4:8], r_rstd[4:8], 4)
                for j in range(4):
                    kb.scale_cast(ckvn[:, j, :], ckv[:, j, 0:256], rstd[:, 4 + j:5 + j], [r_ckv[j], r_rstd[4 + j]],
                                  [r_ckvn[j]])
                dve.op(lambda e: e.tensor_tensor(out=t1[:], in0=ckv[:, :, 256:288], in1=RT[:, :, 0:32], op=ALU.mult),
                       reads=r_ckv + [rRT], writes=[r_t1])
                dve.op(lambda e: e.tensor_tensor(out=t2[:, :, 0:16], in0=ckv[:, :, 272:288], in1=RT[:, :, 32:48],
                                                 op=ALU.mult), reads=r_ckv + [rRT], writes=[r_t2])
                dve.op(lambda e: e.tensor_tensor(out=t2[:, :, 16:32], in0=ckv[:, :, 256:272], in1=RT[:, :, 48:64],
                                                 op=ALU.mult), reads=r_ckv + [rRT], writes=[r_t2])
                dve.op(lambda e: e.tensor_tensor(out=kpe[:], in0=t1[:], in1=t2[:], op=ALU.add), reads=[r_t1, r_t2],
                       writes=[r_kpe])
            yield
            if do_mla:
                for kc in range(2):
                    b = nb()
                    pt, rpt = PB(b)
                    ptb = PBbf(b)
                    for j in range(4):
                        pe.op(lambda e: e.transpose(out=ptb[:, j * 128:(j + 1) * 128],
                                                    in_=ckvn[:, j, kc * 128:(kc + 1) * 128], identity=ident[:]),
                              reads=[r_ckvn[j], r_ident], writes=[rpt], inc=(j == 3))
                    evac(ckvT[:, kc, :], ptb[:, 0:512], r_ckvT[kc], rpt)
                b = nb()
                pt, rpt = PB(b)
                ptb = PBbf(b)
                KPT, rKPT = kpT[s]
                for j in range(4):
                    pe.op(lambda e: e.transpose(out=ptb[0:32, j * 128:(j + 1) * 128], in_=kpe[:, j, :], identity=ident[:]),
                          reads=[r_kpe, r_ident], writes=[rpt], inc=(j == 3))
                evac(KPT[:, :], ptb[0:32, 0:512], rKPT, rpt)
                pool.dma(seq.KP[:, col0:col0 + 512], KPT[:, :], rKPT, seq.r_kv, load=False)
            yield
            if do_mla:
                KNT, rKNT = knT[s]
                for hp in range(4):
                    b = nb()
                    pt, rpt = PB(b)
                    for kc in range(2):
                        pe.op(lambda e: e.matmul(pt[:, :], lhsT=wkvb[:, kc, hp * 128:(hp + 1) * 128], rhs=ckvT[:, kc, :],
                                                 start=(kc == 0), stop=(kc == 1)), reads=[r_wkvb, r_ckvT[kc]], writes=[rpt],
                              inc=(kc == 1))
                    evac(KNT[:, hp, :], pt[:, :], rKNT, rpt)
                pool.dma(seq.KN[:, col0:col0 + 512].rearrange("(c p) n -> p c n", p=128), KNT[:], rKNT, seq.r_kv,
                         load=False)
                VS, rVS = vst[s]
                for j in range(4):
                    b = nb()
                    pt, rpt = PB(b)
                    for kc in range(2):
                        pe.op(lambda e: e.matmul(pt[:, :], lhsT=ckvT[:, kc, j * 128:(j + 1) * 128], rhs=wkvb[:, kc, 512:1024],
                                                 start=(kc == 0), stop=(kc == 1)), reads=[r_wkvb, r_ckvT[kc]], writes=[rpt],
                              inc=(kc == 1))
                    evac(VS[:, :, j, 0:64], pt[:, :].rearrange("p (h d) -> p h d", d=64), rVS, rpt)
                blk0 = col0 // 128
                pool.dma(seq.VD[:, :, blk0:blk0 + 4, :].rearrange("h p j d -> p h j d"), VS[:], rVS, seq.r_kv, load=False)
            if do_na:
                for (dst_t, c_off, dram) in ((nqT[s], 0, seq.NQ), (nkT[s], 512, seq.NK)):
                    T_, rT_ = dst_t
                    for c4 in range(4):
                        b = nb()
                        pt, rpt = PB(b)
                        for f in range(8):
                            pe.op(lambda e: e.matmul(pt[:, :], lhsT=wna[:, f, c_off + c4 * 128:c_off + (c4 + 1) * 128],
                                                     rhs=hT[:, f, :], start=(f == 0), stop=(f == 7)),
                                  reads=[r_wna, r_hT[f]], writes=[rpt], inc=(f == 7))
                        evac(T_[:, c4, :], pt[:, :], rT_, rpt)
                    pool.dma(dram[:, col0:col0 + 512].rearrange("(c p) n -> p c n", p=128), T_[:], rT_, seq.r_na,
                             load=False)
                NVS, rNVS = nvs[s]
                for j in range(4):
                    b = nb()
                    pt, rpt = PB(b)
                    for f in range(8):
                        pe.op(lambda e: e.matmul(pt[:, :], lhsT=hT[:, f, j * 128:(j + 1) * 128], rhs=wna[:, f, 1024:1536],
                                                 start=(f == 0), stop=(f == 7)), reads=[r_wna, r_hT[f]], writes=[rpt],
                              inc=(f == 7))
                    evac(NVS[:, j, :, 0:64], pt[:, :].rearrange("p (h d) -> p h d", d=64), rNVS, rpt)
                pool.dma(seq.NV[col0:col0 + 512, :].rearrange("(j p) c -> p j c", p=128),
                         NVS[:].rearrange("p j h d -> p j (h d)"), rNVS, seq.r_na,
                         load=False)

        p1_lim = debug.get("p1_tiles") if debug else None
        jobs = []
        for si, nm in enumerate(("p0", "p1")):
            for t in range(SP_LEN // 512 if p1_lim is None else p1_lim):
                jobs.append((xp[si, t * 512:(t + 1) * 512, :], ropek[t * 512:(t + 1) * 512, :], seqs[nm], t * 512, True, True))
        if not (debug and debug.get("skip_sample")):
            for t in range(SS_LEN // 512 if p1_lim is None else p1_lim):
                jobs.append((xs[t * 512:(t + 1) * 512, :], ropek[t * 512:(t + 1) * 512, :], seqs["sm"], t * 512, True, False))
            for t in range(XO_LEN // 512 if p1_lim is None else p1_lim):
                jobs.append((xo[t * 512:(t + 1) * 512, :], None, seqs["so"], t * 512, False, True))
        gens = [phase1_tile(*job, ti) for ti, job in enumerate(jobs)]
        next(gens[0])
        if len(gens) > 1:
            next(gens[1])
        next(gens[0])
        next(gens[0])
        for ti in range(len(gens)):
            if ti + 2 < len(gens):
                next(gens[ti + 2])
            if ti + 1 < len(gens):
                next(gens[ti + 1])
            next(gens[ti])
            if ti + 1 < len(gens):
                next(gens[ti + 1])
            for _ in gens[ti]:
                pass
        kb.barrier()

    if debug and debug.get("stop") == 1:
        kb.finish()
        return nc

    with ExitStack() as st2:
        def sb2(name, shape, dt, dma=False):
            t = st2.enter_context(nc.sbuf_tensor(name, list(shape), dt))
            return t, kb.res(name, dma)

        ntab_max = max(n_ptab, n_stab)
        tab, r_tab = sb2("nb_tab", [128, ntab_max, H, 256], BF16, dma=True)

        def load_tabs(kind):
            src, n_ = (ptab, n_ptab) if kind == "p" else (stab, n_stab)
            for t0_ in range(0, n_, 4):
                t1_ = min(t0_ + 4, n_)
                sp.dma(tab[:, t0_:t1_, :, :], src[t0_:t1_].rearrange("t p h q -> p t h q"), r_tab)

        nq = [[sb2("nb_q%d_%d" % (i, par_), [128, 4, 512], BF16, dma=True) for par_ in range(2)] for i in range(2)]
        for i in range(2):
            dve.op(lambda e: e.memset(nq[i][0][0][64:128, :, :], 0.0), writes=[nq[i][0][1]])
            dve.op(lambda e: e.memset(nq[i][1][0][0:64, :, :], 0.0), writes=[nq[i][1][1]])
        nk = [sb2("nb_k%d" % i, [128, 4, 1024], BF16, dma=True) for i in range(2)]
        nv = [sb2("nb_v%d" % i, [128, 8, 1024], BF16, dma=True) for i in range(2)]
        npt = [sb2("nb_pt%d" % i, [128, 1536], BF16) for i in range(2)]
        nob = [sb2("nb_ob%d" % i, [128, 4, 512], BF16, dma=True) for i in range(2)]
        nrec, r_nrec = sb2("nb_rec", [128, 256], F32)
        itc = [0]

        def na_tile(kind, seq, rows, ti, tindex):
            off = 0 if kind == "p" else FOFF
            r0 = rows[0]
            qw = len(rows) * 64
            groups = [rows[i:i + 4] for i in range(0, len(rows), 4)]
            gblks = [na_group_blocks(kind, g_) for g_ in groups]
            a_lo = min(min(b) for b in gblks)
            a_hi = max(max(b) for b in gblks)
            nrow_k = a_hi + 2 - a_lo
            (Qe, rQe), (Qo, rQo) = nq[ti % 2]
            Kt, rK = nk[ti % 2]
            V, rV = nv[ti % 2]
            OBt, rOB = nob[ti % 2]
            qc0 = (r0 + off) * 64
            nq_v = seq.NQ[:, qc0:qc0 + qw].rearrange("(c p) n -> p c n", p=128)
            sp.dma(Qe[0:64, :, 0:qw], nq_v[0:64], rQe, seq.r_na)
            sp.dma(Qo[64:128, :, 0:qw], nq_v[64:128], rQo, seq.r_na)
            sp.dma(Kt[:, :, 0:nrow_k * 64], seq.NK[:, a_lo * 64:(a_hi + 2) * 64].rearrange("(c p) n -> p c n", p=128),
                   rK, seq.r_na)
            sp.dma(V[:, 0:nrow_k // 2, :], seq.NV[a_lo * 64:(a_hi + 2) * 64, :].rearrange("(b p) c -> p b c", p=128),
                   rV, seq.r_na)
            its = [(h, gi) for h in range(H) for gi in range(len(groups))]
            base = itc[0]
            itc[0] += len(its)

            def qkb(k_):
                h, gi = its[k_]
                hp, hc = (h % 2) * 64, h // 2
                g_, blks = groups[gi], gblks[gi]
                par = (base + k_) % 2
                PT, rPT = npt[par]
                nbank = (len(blks) + 1) // 2
                assert nbank <= 3
                sb = [2 + par * 3 + i for i in range(nbank)]
                for bi in range(nbank):
                    pt, rpt = PB(sb[bi])
                    bb = blks[bi * 2:bi * 2 + 2]
                    for ej, a in enumerate(bb):
                        Qm, rQm = (Qe, rQe) if h % 2 == 0 else (Qo, rQo)
                        pe.op(lambda e: e.matmul(pt[:, ej * 256:(ej + 1) * 256],
                                                 lhsT=Kt[:, hc, (a - a_lo) * 64:(a - a_lo) * 64 + 128],
                                                 rhs=Qm[:, hc, gi * 256:(gi + 1) * 256], start=(ej == 0),
                                                 stop=False, skip_group_check=True),
                              reads=[rK, rQm], writes=[rpt], inc=False)
                    for ej, a in enumerate(bb):
                        tid = tindex[(g_[0], a)]
                        pe.op(lambda e: e.matmul(pt[:, ej * 256:(ej + 1) * 256], lhsT=ident[:, :], rhs=tab[:, tid, h, :],
                                                 start=False, stop=True, skip_group_check=True),
                              reads=[r_ident, r_tab], writes=[rpt], inc=(ej == len(bb) - 1))
                    w = len(bb) * 256
                    act.op(lambda e: e.activation(out=PT[:, bi * 512:bi * 512 + w], in_=pt[:, 0:w], func=AF.Exp),
                           reads=[rpt], writes=[rPT])

            def pv(k_):
                h, gi = its[k_]
                hp, hc = (h % 2) * 64, h // 2
                blks = gblks[gi]
                par = (base + k_) % 2
                PT, rPT = npt[par]
                ob_, rob_ = PB(par)
                for bi, a in enumerate(blks):
                    pe.op(lambda e: e.matmul(ob_[:, 0:256], lhsT=V[:, (a - a_lo) // 2, h * 128:(h + 1) * 128],
                                             rhs=PT[:, bi * 256:(bi + 1) * 256], start=(bi == 0),
                                             stop=(bi == len(blks) - 1)),
                          reads=[rV, rPT], writes=[rob_], inc=(bi == len(blks) - 1))
                dve.op(lambda e: e.reciprocal(out=nrec[64:128, :], in_=ob_[64:128, 0:256]), reads=[rob_],
                       writes=[r_nrec])
                dve.op(lambda e: e.tensor_tensor(out=OBt[hp:hp + 64, hc, gi * 256:(gi + 1) * 256], in0=ob_[0:64, 0:256],
                                                 in1=nrec[64:128, :], op=ALU.mult), reads=[rob_, r_nrec], writes=[rOB])

            qkb(0)
            for k_ in range(len(its)):
                if k_ + 1 < len(its):
                    qkb(k_ + 1)
                pv(k_)
            pool.dma(seq.OB[:, qc0:qc0 + qw].rearrange("(c p) n -> p c n", p=128), OBt[:, :, 0:qw], rOB, seq.r_ob, load=False)

        ti = 0
        nb_lim = debug.get("nb_tiles") if debug else None
        stg1 = make_staging(st2, "b", 4)
        prest = prep_rest(stg1, [0])

        def advance(n):
            for _ in range(n):
                if next(prest, "done") == "done":
                    break

        load_tabs("p")
        for nm in ("p0", "p1"):
            for t in range(8 if nb_lim is None else nb_lim):
                na_tile("p", seqs[nm], list(range(t * 8, t * 8 + 8)), ti, ptab_index)
                advance(5)
                ti += 1
        if not (debug and debug.get("skip_sample")):
            load_tabs("s")
            for t in range(4 if nb_lim is None else nb_lim):
                na_tile("s", seqs["so"], list(range(t * 8, t * 8 + 8)), ti, stab_index)
                advance(5)
                ti += 1
            for rr in (-4, OWN // GRID_W):
                na_tile("s", seqs["so"], list(range(rr, rr + 4)), ti, stab_index)
                ti += 1
        advance(10000)
        kb.barrier()

    if debug and debug.get("stop") == 2:
        kb.finish()
        return nc

    with ExitStack() as st3:
        def sb3(name, shape, dt, dma=False):
            t = st3.enter_context(nc.sbuf_tensor(name, list(shape), dt))
            return t, kb.res(name, dma)

        NR = 4
        ring = [sb3("w_ring%d" % i, [128, 8192], BF16, dma=True) for i in range(NR)]
        NX = 2
        Xb = []
        for i in range(NX):
            t_, _ = sb3("w_x%d" % i, [128, 4, D], F32)
            Xb.append((t_, [kb.res("w_x%d_%d" % (i, j), dma=True) for j in range(4)]))
        Xst = [[kb.res("w_xst%d_%d" % (i, j), dma=True) for j in range(4)] for i in range(NX)]
        xn, _ = sb3("w_xn", [128, 4, D], BF16)
        r_xn = [kb.res("w_xn%d" % j) for j in range(4)]
        hT, _ = sb3("w_hT", [128, 8, 512], BF16)
        r_hT = [kb.res("w_hT%d" % j) for j in range(4)]
        big, _ = sb3("w_big", [128, NFF * 512], BF16)
        aT = big[:, :].rearrange("p (c n) -> p c n", n=512)
        r_aT = kb.res("w_aT")
        cqn = big[:, 0:4 * 768].rearrange("p (j n) -> p j n", n=768)
        r_cqn = kb.res("w_cqn")
        cqnT = big[:, 3072:3072 + 6 * 512].rearrange("p (c n) -> p c n", n=512)
        r_cqnT = kb.res("w_cqnT")
        qsb = big[:, 6144:6144 + 8 * 512].rearrange("p (h n) -> p h n", n=512)
        r_qsb = kb.res("w_qsb")
        r_aT.alias = [r_cqn, r_cqnT, r_qsb]
        for r_ in (r_cqn, r_cqnT, r_qsb):
            r_.alias = [r_aT]
        rq = [sb3("w_rq%d" % i, [64, 512], F32, dma=True) for i in range(2)]
        oaT, r_oaT = sb3("w_oaT", [128, 4, 512], BF16)
        obT = [sb3("w_obT%d" % i, [128, 4, 512], BF16, dma=True) for i in range(1)]
        mT, r_mT = sb3("w_mT", [128, 8, 512], BF16)
        NKV = 4
        kbuf = [sb3("w_kb%d" % i, [128, KC], BF16, dma=True) for i in range(NKV)]
        vbuf = [sb3("w_vb%d" % i, [128, NBLK, 128], BF16, dma=True) for i in range(NKV)]
        rec, r_rec = sb3("w_rec", [128, 512], F32)
        tg = [sb3("w_tg%d" % i, [128, 512], F32) for i in range(2)]
        tm = [sb3("w_tm%d" % i, [128, 512], F32) for i in range(2)]
        tcv = [sb3("w_tc%d" % i, [128, 512], F32) for i in range(2)]
        th = [sb3("w_th%d" % i, [128, 512], F32) for i in range(2)]

        bank_rr = [0]

        def nb():
            bank_rr[0] = (bank_rr[0] + 1) % 8
            return bank_rr[0]

        evac_rr = [0]

        def evac(out_ap, in_ap, r_out, r_in):
            evac_rr[0] += 1
            if evac_rr[0] % 2:
                act.op(lambda e: e.activation(out=out_ap, in_=in_ap, func=AF.Copy), reads=[r_in], writes=[r_out])
            else:
                dve.op(lambda e: e.tensor_copy(out=out_ap, in_=in_ap), reads=[r_in], writes=[r_out])

        def piece_defs():
            d = [("cq", [(WIN[:, 0:768], 8, 768)]),
                 ("qb", [(WQB[:, :], 6, 1024)]),
                 ("br", [(WBA[:, :], 4, 1024), (WBB[:, :], 4, 1024)])]
            for g in range(2):
                d.append(("g%d" % g, [(WIN[:, 2592 + g * 512:2592 + (g + 1) * 512], 8, 512),
                                      (WIN[:, 3616 + g * 512:3616 + (g + 1) * 512], 8, 512)]))
            d.append(("o", [(WO[:, :], 8, 1024)]))
            for g in range(6):
                w = 512 if g < 5 else 256
                d.append(("up%d" % g, [(WUP[:, g * 512:g * 512 + w], 8, w), (WUP[:, DFF + g * 512:DFF + g * 512 + w], 8, w)]))
            for g in range(3):
                k0, k1 = g * 8, min(g * 8 + 8, NFF)
                d.append(("dn%d" % g, [(WDN[k0 * 128:k1 * 128, :], k1 - k0, 1024)]))
            return d

        PDEF = piece_defs()
        NPIECE = len(PDEF)
        pstate = {"issued": 0, "total": 0, "released": 0}

        def release_piece(win_i, name):
            li = [n for n, _ in PDEF].index(name)
            gidx = win_i * NPIECE + li
            pstate["released"] = max(pstate["released"], gidx + 1)

        def issue_piece(gidx):
            name, parts = PDEF[gidx % NPIECE]
            t, r = ring[gidx % NR]
            o = 0
            for (src, kcn, w) in parts:
                sp.dma(t[:, o:o + kcn * w].rearrange("p (c n) -> p c n", n=w), src.rearrange("(c p) n -> p c n", p=128), r, r_w)
                o += kcn * w

        def get_piece(win_i, name, ahead=2):
            li = [n for n, _ in PDEF].index(name)
            gidx = win_i * NPIECE + li
            while (pstate["issued"] <= min(gidx + ahead, pstate["total"] - 1)
                   and pstate["issued"] - NR < pstate["released"]):
                issue_piece(pstate["issued"])
                pstate["issued"] += 1
            assert pstate["issued"] > gidx, (name, gidx, pstate)
            t, r = ring[gidx % NR]
            views, o = [], 0
            for (src, kcn, w) in PDEF[li][1]:
                views.append(t[:, o:o + kcn * w].rearrange("p (c n) -> p c n", n=w))
                o += kcn * w
            return views, r

        def rstd_of(ss_ap, out_ap, n, np_=128):
            act.op(lambda e: e.activation(out=out_ap, in_=ss_ap, func=AF.Sqrt, bias=epsT[0:np_, 0:1], scale=1.0 / n),
                   reads=[r_ss, r_eps], writes=[r_rstd])
            dve.op(lambda e: e.reciprocal(out=out_ap, in_=out_ap), reads=[r_rstd], writes=[r_rstd])

        P2 = [sb3("w_p2_%d" % i, [128, 2, 512], BF16) for i in range(3)]
        sst = {k_: (sb3("w_ss_" + k_, [128, 8], F32)[0], [kb.res("w_ss_%s%d" % (k_, c_)) for c_ in range(8)])
               for k_ in ("h", "q", "n2", "f")}
        rst = {k_: (sb3("w_rs_" + k_, [128, 8], F32)[0], [kb.res("w_rs_%s%d" % (k_, c_)) for c_ in range(8)])
               for k_ in ("h", "q", "n2", "f")}
        junks = [sb3("w_junk%d" % i, [128, D], BF16) for i in range(3)]
        jrr = [0]

        def jk():
            jrr[0] += 1
            return junks[jrr[0] % 3]

        def rstd2(tag, col, np_, n):
            s_, rs_ = sst[tag]
            o_, ro_ = rst[tag]
            act.op(lambda e: e.activation(out=o_[0:np_, col:col + 1], in_=s_[0:np_, col:col + 1], func=AF.Sqrt,
                                          bias=epsT[0:np_, 0:1], scale=1.0 / n), reads=[rs_[col], r_eps], writes=[ro_[col]])
            dve.op(lambda e: e.reciprocal(out=o_[0:np_, col:col + 1], in_=o_[0:np_, col:col + 1]), reads=[ro_[col]],
                   writes=[ro_[col]])

        def norm_a(st, tag):
            X, rX = st["X"]
            s_, rs_ = sst[tag]
            o_, ro_ = rst[tag]
            for j, sz in enumerate(st["subs"]):
                jt, rj = jk()
                act.op(lambda e: e.activation(out=jt[0:sz, :], in_=X[0:sz, j, :], func=AF.Square,
                                              accum_out=s_[0:sz, j:j + 1]), reads=[rX[j]], writes=[rj, rs_[j]])
                rstd2(tag, j, sz, D)
                kb.scale_cast(xn[0:sz, j, :], X[0:sz, j, :], o_[0:sz, j:j + 1], [rX[j], ro_[j]], [r_xn[j]])

        def norm_b(st):
            for j, sz in enumerate(st["subs"]):
                b = nb()
                pt, rpt = PB(b)
                ptb = PBbf(b)
                for f in range(8):
                    pe.op(lambda e: e.transpose(out=ptb[:, f * 128:f * 128 + sz], in_=xn[0:sz, j, f * 128:(f + 1) * 128],
                                                identity=ident[0:sz, 0:sz]), reads=[r_xn[j], r_ident], writes=[rpt],
                          inc=(f == 7))
                evac(hT[:, :, j * 128:j * 128 + sz], ptb.rearrange("p (f n) -> p f n", n=128)[:, :, 0:sz], r_hT[j], rpt)

        kvstate = {"n": 0}

        def w_state(wi, W):
            ncols = W["ncols"]
            st = dict(W)
            st["wi"] = wi
            st["subs"] = [min(128, ncols - j * 128) for j in range((ncols + 127) // 128)]
            st["X"] = Xb[wi % NX]
            st["RQ"] = rq[wi % 2]
            return st

        def s_load(st):
            ncols = st["ncols"]
            X, rX = st["X"]
            for j, sz in enumerate(st["subs"]):
                sp.dma(X[0:sz, j, :], st["x"][j * 128:j * 128 + sz, :], rX[j])
            RQ, rRQ = st["RQ"]
            sp.dma(RQ[:, 0:ncols], st["ropeq"], rRQ)

        def s_cq(st):
            wi, ncols, subs = st["wi"], st["ncols"], st["subs"]
            OBT, rOBT = obT[0]
            sp.dma(OBT[:, :, 0:ncols], st["ob"].rearrange("(c p) n -> p c n", p=128), rOBT, st["r_ob"])
            (wcq,), r_wcq = get_piece(wi, "cq")
            s_, rs_ = sst["q"]
            o_, ro_ = rst["q"]
            for j, sz in enumerate(subs):
                bks = []
                for half in range(2):
                    b = nb()
                    pt, rpt = PB(b)
                    bks.append((pt, rpt))
                    for f in range(8):
                        pe.op(lambda e: e.matmul(pt[0:sz, 0:384], lhsT=hT[:, f, j * 128:j * 128 + sz],
                                                 rhs=wcq[:, f, half * 384:(half + 1) * 384], start=(f == 0), stop=(f == 7)),
                              reads=[r_hT[j], r_wcq], writes=[rpt], inc=(f == 7))
                    jt, rj = jk()
                    act.op(lambda e: e.activation(out=jt[0:sz, 0:384], in_=pt[0:sz, 0:384], func=AF.Square,
                                                  accum_out=s_[0:sz, half:half + 1]), reads=[rpt], writes=[rj, rs_[half]])
                dve.op(lambda e: e.tensor_tensor(out=s_[0:sz, 2:3], in0=s_[0:sz, 0:1], in1=s_[0:sz, 1:2], op=ALU.add),
                       reads=[rs_[0], rs_[1]], writes=[rs_[2]])
                rstd2("q", 2, sz, 768)
                for half in range(2):
                    pt, rpt = bks[half]
                    if half == 0:
                        act.op(lambda e: e.activation(out=cqn[0:sz, j, 0:384], in_=pt[0:sz, 0:384], func=AF.Copy,
                                                      scale=o_[0:sz, 2:3]), reads=[rpt, ro_[2]], writes=[r_cqn])
                    else:
                        dve.op(lambda e: e.tensor_scalar(out=cqn[0:sz, j, 384:768], in0=pt[0:sz, 0:384],
                                                         scalar1=o_[0:sz, 2:3], scalar2=None, op0=ALU.mult),
                               reads=[rpt, ro_[2]], writes=[r_cqn])
            for kc in range(6):
                b = nb()
                pt, rpt = PB(b)
                ptb = PBbf(b)
                for j, sz in enumerate(subs):
                    pe.op(lambda e: e.transpose(out=ptb[:, j * 128:j * 128 + sz], in_=cqn[0:sz, j, kc * 128:(kc + 1) * 128],
                                                identity=ident[0:sz, 0:sz]), reads=[r_cqn, r_ident], writes=[rpt],
                          inc=(j == len(subs) - 1))
                evac(cqnT[:, kc, 0:ncols], ptb[:, 0:ncols], r_cqnT, rpt)
            release_piece(wi, "cq")

        def s_q(st):
            wi, ncols = st["wi"], st["ncols"]
            RQ, rRQ = st["RQ"]
            (wqb,), r_wqb = get_piece(wi, "qb")
            for h in range(H):
                bm = nb()
                pm, rpm = PB(bm)
                for kc in range(6):
                    pe.op(lambda e: e.matmul(pm[0:96, 0:ncols], lhsT=wqb[:, kc, h * 128:h * 128 + 96], rhs=cqnT[:, kc, 0:ncols],
                                             start=(kc == 0), stop=(kc == 5)), reads=[r_wqb, r_cqnT], writes=[rpm],
                          inc=(kc == 5))
                bs = nb()
                psw, rps = PB(bs)
                for kc in range(6):
                    pe.op(lambda e: e.matmul(psw[0:32, 0:ncols], lhsT=wqb[:, kc, h * 128 + 96:h * 128 + 128],
                                             rhs=cqnT[:, kc, 0:ncols], start=(kc == 0), stop=(kc == 5)),
                          reads=[r_wqb, r_cqnT], writes=[rps], inc=(kc == 5))
                act.op(lambda e: e.activation(out=qsb[0:64, h, 0:ncols], in_=pm[0:64, 0:ncols], func=AF.Copy),
                       reads=[rpm], writes=[r_qsb])
                t1, r_t1 = tm[0]
                t2, r_t2 = tm[1]
                dve.op(lambda e: e.tensor_tensor(out=t1[0:32, 0:ncols], in0=pm[64:96, 0:ncols], in1=RQ[0:32, 0:ncols],
                                                 op=ALU.mult), reads=[rpm, rRQ], writes=[r_t1])
                dve.op(lambda e: e.tensor_tensor(out=t2[0:32, 0:ncols], in0=psw[0:32, 0:ncols], in1=RQ[32:64, 0:ncols],
                                                 op=ALU.mult), reads=[rps, rRQ], writes=[r_t2])
                dve.op(lambda e: e.tensor_tensor(out=qsb[64:96, h, 0:ncols], in0=t1[0:32, 0:ncols], in1=t2[0:32, 0:ncols],
                                                 op=ALU.add), reads=[r_t1, r_t2], writes=[r_qsb])
            release_piece(wi, "qb")

        def s_attn(st):
            ncols = st["ncols"]
            kv = st["kv"]
            nch = st["S_kv"] // KC
            units = [(h, ch) for h in range(H) for ch in range(nch)]

            def load_chunk(ui):
                h, ch = units[ui]
                s_ = (kvstate["n"] + ui) % NKV
                KBt, rKB = kbuf[s_]
                VBt, rVB = vbuf[s_]
                sp.dma(KBt[0:64, :], kv.KN[h * 64:(h + 1) * 64, ch * KC:(ch + 1) * KC], rKB, kv.r_kv)
                sp.dma(KBt[64:96, :], kv.KP[:, ch * KC:(ch + 1) * KC], rKB, kv.r_kv)
                pool.dma(VBt[:, :, :], kv.VD[h, :, ch * NBLK:(ch + 1) * NBLK, :], rVB, kv.r_kv)

            nload = 0
            for ui in range(min(NKV - 1, len(units))):
                load_chunk(ui)
                nload += 1
            NPB = NBLK // 2
            pairs = [(ui, pb_) for ui in range(len(units)) for pb_ in range(NPB)]

            def emit_qk(i):
                ui, pb_ = pairs[i]
                h, ch = units[ui]
                KBt, rKB = kbuf[(kvstate["n"] + ui) % NKV]
                g = 1 + i % 3
                for t_ in range(2):
                    pt, rpt = PB(2 * g + t_)
                    b_ = 2 * pb_ + t_
                    pe.op(lambda e: e.matmul(pt[:, 0:ncols], lhsT=KBt[0:96, b_ * 128:(b_ + 1) * 128],
                                             rhs=qsb[0:96, h, 0:ncols], start=True, stop=True), reads=[rKB, r_qsb],
                          writes=[rpt], inc=(t_ == 1))

            emit_qk(0)
            emit_qk(1)
            for i, (ui, pb_) in enumerate(pairs):
                h, ch = units[ui]
                if pb_ == 0 and nload < len(units) and nload <= ui + NKV - 2:
                    load_chunk(nload)
                    nload += 1
                if i + 2 < len(pairs):
                    emit_qk(i + 2)
                VBt, rVB = vbuf[(kvstate["n"] + ui) % NKV]
                g = 1 + i % 3
                P_, rP = P2[i % 3]
                sview = pdbl[g][:, :].rearrange("p (b n) -> p b n", b=2)[:, :, 0:ncols]
                act.op(lambda e: e.activation(out=P_[:, :, 0:ncols], in_=sview, func=AF.Exp),
                       reads=[PB(2 * g)[1], PB(2 * g + 1)[1]], writes=[rP])
                ob_, rob_ = PB(h % 2)
                for t_ in range(2):
                    b_ = 2 * pb_ + t_
                    first = (ch == 0 and b_ == 0)
                    lastb = (ch == nch - 1 and b_ == NBLK - 1)
                    pe.op(lambda e: e.matmul(ob_[:, 0:ncols], lhsT=VBt[:, b_, :], rhs=P_[:, t_, 0:ncols], start=first,
                                             stop=lastb), reads=[rVB, rP], writes=[rob_], inc=(t_ == 1))
                if ch == nch - 1 and pb_ == NPB - 1:
                    dve.op(lambda e: e.reciprocal(out=rec[64:128, 0:ncols], in_=ob_[64:128, 0:ncols]), reads=[rob_],
                           writes=[r_rec])
                    hp, hc = (h % 2) * 64, h // 2
                    dve.op(lambda e: e.tensor_tensor(out=oaT[hp:hp + 64, hc, 0:ncols], in0=ob_[0:64, 0:ncols],
                                                     in1=rec[64:128, 0:ncols], op=ALU.mult), reads=[rob_, r_rec],
                           writes=[r_oaT])
            kvstate["n"] += len(units)

        def s_gates(st):
            wi, ncols = st["wi"], st["ncols"]
            OBT, rOBT = obT[0]
            (wba, wbb), r_wbr = get_piece(wi, "br")
            for m in range(8):
                (wga, wgb), r_wg = get_piece(wi, "g%d" % (m // 4))
                mm = m % 4
                tgs = []
                for gi, wg_ in enumerate((wga, wgb)):
                    b = nb()
                    pt, rpt = PB(b)
                    for f in range(8):
                        pe.op(lambda e: e.matmul(pt[:, 0:ncols], lhsT=wg_[:, f, mm * 128:(mm + 1) * 128], rhs=hT[:, f, 0:ncols],
                                                 start=(f == 0), stop=(f == 7)), reads=[r_wg] + r_hT, writes=[rpt],
                              inc=(f == 7))
                    tg_, rtg = tg[gi]
                    act.op(lambda e: e.activation(out=tg_[:, 0:ncols], in_=pt[:, 0:ncols], func=AF.Tanh,
                                                  bias=bg[:, gi * 8 + m:gi * 8 + m + 1], scale=0.5), reads=[rpt, r_bg],
                           writes=[rtg])
                    tgs.append((tg_, rtg))
                for gi, (wb_, src, rsrc) in enumerate(((wba, oaT, r_oaT), (wbb, OBT, rOBT))):
                    b = nb()
                    pt, rpt = PB(b)
                    for kc in range(4):
                        pe.op(lambda e: e.matmul(pt[:, 0:ncols], lhsT=wb_[:, kc, m * 128:(m + 1) * 128], rhs=src[:, kc, 0:ncols],
                                                 start=(kc == 0), stop=(kc == 3)), reads=[r_wbr, rsrc], writes=[rpt],
                              inc=(kc == 3))
                    tg_, rtg = tgs[gi]
                    tm_, rtm = tm[gi]
                    dve.op(lambda e: e.scalar_tensor_tensor(out=tm_[:, 0:ncols], in0=tg_[:, 0:ncols], scalar=1.0,
                                                            in1=pt[:, 0:ncols], op0=ALU.add, op1=ALU.mult),
                           reads=[rtg, rpt], writes=[rtm])
                dve.op(lambda e: e.tensor_tensor(out=mT[:, m, 0:ncols], in0=tm[0][0][:, 0:ncols], in1=tm[1][0][:, 0:ncols],
                                                 op=ALU.add), reads=[tm[0][1], tm[1][1]], writes=[r_mT])
            release_piece(wi, "g1")

        def s_wo_norm2(st):
            wi, ncols, subs = st["wi"], st["ncols"], st["subs"]
            X, rX = st["X"]
            (wo,), r_wo = get_piece(wi, "o")
            s_, rs_ = sst["n2"]
            o_, ro_ = rst["n2"]
            for j, sz in enumerate(subs):
                for half in range(2):
                    b = nb()
                    pt, rpt = PB(b)
                    for kc in range(8):
                        pe.op(lambda e: e.matmul(pt[0:sz, :], lhsT=mT[:, kc, j * 128:j * 128 + sz],
                                                 rhs=wo[:, kc, half * 512:(half + 1) * 512], start=(kc == 0), stop=(kc == 7)),
                              reads=[r_wo, r_mT], writes=[rpt], inc=(kc == 7))
                    dve.op(lambda e: e.tensor_tensor(out=X[0:sz, j, half * 512:(half + 1) * 512], in0=pt[0:sz, :],
                                                     in1=X[0:sz, j, half * 512:(half + 1) * 512], op=ALU.add),
                           reads=[rpt, rX[j]], writes=[rX[j]])
                jt, rj = jk()
                act.op(lambda e: e.activation(out=jt[0:sz, :], in_=X[0:sz, j, :], func=AF.Square,
                                              accum_out=s_[0:sz, j:j + 1]), reads=[rX[j]], writes=[rj, rs_[j]])
                rstd2("n2", j, sz, D)
                kb.scale_cast(xn[0:sz, j, :], X[0:sz, j, :], o_[0:sz, j:j + 1], [rX[j], ro_[j]], [r_xn[j]])
            release_piece(wi, "o")
            norm_b(st)
            if st.get("maskL"):
                dve.op(lambda e: e.tensor_scalar(out=hT[:, :, 0:1], in0=hT[:, :, 0:1], scalar1=hm[:, 0:1], scalar2=None,
                                                 op0=ALU.mult), reads=[r_hT[0], r_hm], writes=[r_hT[0]])
            if st.get("maskR"):
                dve.op(lambda e: e.tensor_scalar(out=hT[:, :, ncols - 1:ncols], in0=hT[:, :, ncols - 1:ncols],
                                                 scalar1=hm[:, 1:2], scalar2=None, op0=ALU.mult),
                       reads=[r_hT[len(subs) - 1], r_hm], writes=[r_hT[len(subs) - 1]])

        def s_up(st):
            wi, ncols = st["wi"], st["ncols"]
            for c in range(NFF):
                (wu, wg_), r_wup = get_piece(wi, "up%d" % (c // 4))
                cc = c % 4
                bu = nb()
                pu, rpu = PB(bu)
                for f in range(8):
                    pe.op(lambda e: e.matmul(pu[:, 0:ncols], lhsT=wu[:, f, cc * 128:(cc + 1) * 128], rhs=hT[:, f, 0:ncols],
                                             start=(f == 0), stop=(f == 7)), reads=[r_wup] + r_hT, writes=[rpu], inc=(f == 7))
                bgk = nb()
                pg, rpg = PB(bgk)
                for f in range(8):
                    pe.op(lambda e: e.matmul(pg[:, 0:ncols], lhsT=wg_[:, f, cc * 128:(cc + 1) * 128], rhs=hT[:, f, 0:ncols],
                                             start=(f == 0), stop=(f == 7)), reads=[r_wup] + r_hT, writes=[rpg], inc=(f == 7))
                tc_, rtc = tcv[c % 2]
                th_, rth = th[c % 2]
                act.op(lambda e: e.activation(out=tc_[:, 0:ncols], in_=pg[:, 0:ncols], func=AF.Identity,
                                              bias=cwh[:, 3 * NFF + c:3 * NFF + c + 1], scale=cwh[:, NFF + c:NFF + c + 1]),
                       reads=[rpg, r_cwh], writes=[rtc])
                dve.op(lambda e: e.scalar_tensor_tensor(out=tc_[:, 1:ncols], in0=pg[:, 0:ncols - 1], scalar=cwh[:, c:c + 1],
                                                        in1=tc_[:, 1:ncols], op0=ALU.mult, op1=ALU.add),
                       reads=[rpg, rtc, r_cwh], writes=[rtc])
                dve.op(lambda e: e.scalar_tensor_tensor(out=tc_[:, 0:ncols - 1], in0=pg[:, 1:ncols],
                                                        scalar=cwh[:, 2 * NFF + c:2 * NFF + c + 1], in1=tc_[:, 0:ncols - 1],
                                                        op0=ALU.mult, op1=ALU.add), reads=[rpg, rtc, r_cwh], writes=[rtc])
                act.op(lambda e: e.activation(out=th_[:, 0:ncols], in_=tc_[:, 0:ncols], func=AF.Silu), reads=[rtc],
                       writes=[rth])
                dve.op(lambda e: e.tensor_tensor(out=aT[:, c, 0:ncols], in0=pu[:, 0:ncols], in1=th_[:, 0:ncols], op=ALU.mult),
                       reads=[rpu, rth], writes=[r_aT])
                if c % 4 == 3 or c == NFF - 1:
                    release_piece(wi, "up%d" % (c // 4))

        def s_down_final(st):
            wi, ncols, subs, L, n_out = st["wi"], st["ncols"], st["subs"], st["L"], st["n_out"]
            X, rX = st["X"]
            dn = [get_piece(wi, "dn%d" % g) for g in range(3)]
            s_, rs_ = sst["f"]
            o_, ro_ = rst["f"]
            for j, sz in enumerate(subs):
                for half in range(2):
                    b = nb()
                    pt, rpt = PB(b)
                    for c in range(NFF):
                        (wd,), r_wd = dn[c // 8]
                        pe.op(lambda e: e.matmul(pt[0:sz, :], lhsT=aT[:, c, j * 128:j * 128 + sz],
                                                 rhs=wd[:, c % 8, half * 512:(half + 1) * 512], start=(c == 0),
                                                 stop=(c == NFF - 1)), reads=[r_wd, r_aT], writes=[rpt], inc=(c == NFF - 1))
                    dve.op(lambda e: e.tensor_tensor(out=X[0:sz, j, half * 512:(half + 1) * 512], in0=pt[0:sz, :],
                                                     in1=X[0:sz, j, half * 512:(half + 1) * 512], op=ALU.add),
                           reads=[rpt, rX[j]], writes=[rX[j]])
                jt, rj = jk()
                act.op(lambda e: e.activation(out=jt[0:sz, :], in_=X[0:sz, j, :], func=AF.Square,
                                              accum_out=s_[0:sz, j:j + 1]), reads=[rX[j]], writes=[rj, rs_[j]])
                rstd2("f", j, sz, D)
                dve.op(lambda e: e.scalar_tensor_tensor(out=X[0:sz, j, :], in0=X[0:sz, j, :], scalar=o_[0:sz, j:j + 1],
                                                        in1=nfb[0:sz, :], op0=ALU.mult, op1=ALU.mult),
                       reads=[rX[j], ro_[j], r_nfb], writes=[rX[j]])
                p0, p1 = max(L - 128 * j, 0), min(L + n_out - 128 * j, sz)
                if p1 > p0:
                    o0 = 128 * j + p0 - L
                    pool.dma(st["out"][o0:o0 + (p1 - p0), :], X[p0:p1, j, :], rX[j], None, load=False, semres=Xst[wi % NX][j])
            release_piece(wi, "dn2")

        wins = []
        for si, nm in enumerate(("p0", "p1")):
            for (o_s, o_e) in split_windows(SP_LEN):
                c_s, c_e = max(o_s - 1, 0), min(o_e + 1, SP_LEN)
                wins.append(dict(ncols=c_e - c_s, L=o_s - c_s, n_out=o_e - o_s, x=xp[si, c_s:c_e, :],
                                 ropeq=ropeq_p[:, c_s:c_e], ob=seqs[nm].OB[:, c_s:c_e], r_ob=seqs[nm].r_ob, kv=seqs[nm],
                                 S_kv=SP_LEN, out=yp[si, o_s:o_e, :]))
        if not (debug and debug.get("skip_sample")):
            sw = split_windows(OWN)
            for k_, (o_s, o_e) in enumerate(sw):
                c_s, c_e = HALO_LO + o_s - 1, HALO_LO + o_e + 1
                wins.append(dict(ncols=c_e - c_s, L=1, n_out=o_e - o_s, x=xo[c_s:c_e, :], ropeq=ropeq_s[:, c_s:c_e],
                                 ob=seqs["so"].OB[:, c_s:c_e], r_ob=seqs["so"].r_ob, kv=seqs["sm"], S_kv=SS_LEN,
                                 out=yo[o_s:o_e, :], maskL=(k_ == 0), maskR=(k_ == len(sw) - 1)))
        if debug and debug.get("win_list") is not None:
            wins = [wins[i] for i in debug["win_list"]]
        pstate["total"] = len(wins) * NPIECE
        sts = [w_state(wi, W) for wi, W in enumerate(wins)]
        s_load(sts[0])
        norm_a(sts[0], "h")
        norm_b(sts[0])
        for wi, st in enumerate(sts):
            nxt = sts[wi + 1] if wi + 1 < len(sts) else None
            s_cq(st)
            s_q(st)
            if nxt is not None:
                s_load(nxt)
            s_attn(st)
            s_gates(st)
            s_wo_norm2(st)
            s_up(st)
            if nxt is not None:
                norm_a(nxt, "h")
            s_down_final(st)
            if nxt is not None:
                norm_b(nxt)
        kb.barrier()

    kb.finish()
    return nc


def _rope_tables():
    inv = (np.float32(10000.0) ** (-np.arange(0, 32, 2, dtype=np.float32) / np.float32(32))).astype(np.float32)
    ang = (np.arange(SS_LEN, dtype=np.float32)[:, None] * inv[None, :]).astype(np.float32)
    c, s = np.cos(ang).astype(np.float32), np.sin(ang).astype(np.float32)
    return c, s


def make_inputs(inputs, debug=None):
    f = lambda a: np.ascontiguousarray(np.asarray(a, dtype=np.float32))
    x_prompt, x_sample = f(inputs["x_prompt"]), f(inputs["x_sample"])[0]
    c, s = _rope_tables()
    ropek = np.concatenate([c, c, -s, s], axis=1).astype(np.float32)
    ropeq_full = np.ascontiguousarray(ropek.T)
    rpb = f(inputs["rpb"])[0]
    ptab, pidx = na_tables(rpb, "p", 0)
    xs_pad = np.zeros((SS_LEN + 2 * 512, D), np.float32)
    xs_pad[512:512 + SS_LEN] = x_sample
    rq_pad = np.zeros((64, SS_LEN + 2 * 512), np.float32)
    rq_pad[:, 512:512 + SS_LEN] = ropeq_full
    shared = dict(
        xs=x_sample,
        w_in=f(inputs["w_in"])[0], w_qb=f(inputs["w_qb"])[0], w_kvb=f(inputs["w_kvb"])[0],
        w_br_a=f(inputs["w_br_a"])[0], w_br_b=f(inputs["w_br_b"])[0], w_o=f(inputs["w_o"])[0],
        w_up=f(inputs["w_up"])[0], w_down=f(inputs["w_down"])[0],
        norm_mix=f(inputs["norm_mix"])[0], norm_q=f(inputs["norm_q"])[0], norm_kv=f(inputs["norm_kv"])[0],
        norm_ffn=f(inputs["norm_ffn"])[0], norm_final=f(inputs["norm_final"]),
        b_gate=f(inputs["b_gate"])[0], conv_w=f(inputs["conv_w"])[0], conv_b=f(inputs["conv_b"])[0],
        ropek=ropek, ropeq_p=np.ascontiguousarray(ropeq_full[:, :SP_LEN]), ptab=ptab,
        ident=np.eye(128, dtype=np.float32).astype(ml_dtypes.bfloat16),
    )
    in_maps, sidx = [], None
    n_stab = None
    stabs = []
    for c_ in range(NCORES):
        st, si = na_tables(rpb, "s", c_)
        stabs.append(st)
        sidx = si if sidx is None else sidx
    n_stab = max(t.shape[0] for t in stabs)
    for c_ in range(NCORES):
        o0 = c_ * OWN
        st = stabs[c_]
        if st.shape[0] < n_stab:
            st = np.concatenate([st, np.zeros((n_stab - st.shape[0],) + st.shape[1:], st.dtype)], 0)
        hmask = np.ones((128, 2), np.float32)
        if c_ == 0:
            hmask[:, 0] = 0.0
        if c_ == NCORES - 1:
            hmask[:, 1] = 0.0
        m = dict(shared)
        m.update(
            xp=np.ascontiguousarray(x_prompt[2 * c_:2 * c_ + 2]),
            xo=np.ascontiguousarray(xs_pad[512 + o0 - HALO_LO:512 + o0 - HALO_LO + XO_LEN]),
            ropeq_s=np.ascontiguousarray(rq_pad[:, 512 + o0 - HALO_LO:512 + o0 - HALO_LO + XO_LEN]),
            stab=st, hmask=hmask,
        )
        in_maps.append(m)
    return in_maps, (ptab.shape[0], pidx, n_stab, sidx)


def kernel(**inputs):
    in_maps, (n_ptab, pidx, n_stab, sidx) = make_inputs(inputs)
    nc = build_program(n_ptab, pidx, n_stab, sidx)
    res = run_bass_kernel_spmd(nc, in_maps, core_ids=list(range(NCORES)))
    yp = np.concatenate([r["yp"] for r in res.results], axis=0)
    yo = np.concatenate([r["yo"] for r in res.results], axis=0)[None]
    return (np.ascontiguousarray(yp, dtype=np.float32), np.ascontiguousarray(yo, dtype=np.float32))
```
